# Optimizing a Trainium2 kernel written in Bass

```python
import jax, jax.numpy as jnp
from jax import lax
import numpy as np

D_MODEL = 1024
BATCH = 8
SEQ = 4096
DEPTH = 2

N_A_LAYERS = DEPTH // 2
N_B_LAYERS = DEPTH - N_A_LAYERS
D_FF = 2816
D_RNN = 1344
N_LRU_BLOCKS = 16
LRU_BLOCK = D_RNN // N_LRU_BLOCKS
CONV_WIDTH = 4
LRU_C = 8.0
N_HEADS = 16
HEAD_DIM = 64
D_ATTN = N_HEADS * HEAD_DIM
Q_BLOCK = 128
EPS = 1e-6

kernel_name = "yoco_rglru_forgetting_attention_macaron"


def rms_norm(x, g):
    xf = x.astype(jnp.float32)
    y = xf * lax.rsqrt(jnp.mean(xf * xf, axis=-1, keepdims=True) + EPS)
    return (y * g.astype(jnp.float32)).astype(x.dtype)


def swiglu(x, w_gate, w_up, w_down):
    return (jax.nn.silu(x @ w_gate) * (x @ w_up)) @ w_down


def causal_depthwise_conv(x, w, b):
    S = x.shape[1]
    xp = jnp.pad(x, ((0, 0), (CONV_WIDTH - 1, 0), (0, 0)))
    y = b
    for k in range(CONV_WIDTH):
        y = y + xp[:, k:k + S] * w[k]
    return y


def rg_lru(x, w_a, b_a, w_x, b_x, lam):
    Bn, S, _ = x.shape
    xb = x.reshape(Bn, S, N_LRU_BLOCKS, LRU_BLOCK)
    r = jax.nn.sigmoid(jnp.einsum('bsnc,ncd->bsnd', xb, w_a).reshape(Bn, S, D_RNN) + b_a)
    i = jax.nn.sigmoid(jnp.einsum('bsnc,ncd->bsnd', xb, w_x).reshape(Bn, S, D_RNN) + b_x)
    log_a = -LRU_C * r.astype(jnp.float32) * jax.nn.softplus(-lam.astype(jnp.float32))
    a = jnp.exp(log_a)
    u = jnp.sqrt(-jnp.expm1(2.0 * log_a)) * (i * x).astype(jnp.float32)

    def combine(c1, c2):
        a1, b1 = c1
        a2, b2 = c2
        return a1 * a2, a2 * b1 + b2

    _, h = lax.associative_scan(combine, (a, u), axis=1)
    return h.astype(x.dtype)


def recurrent_block(x, w_in, conv_w, conv_b, w_a, b_a, w_x, b_x, lam, w_out):
    gx = x @ w_in
    gate, rec = gx[..., :D_RNN], gx[..., D_RNN:]
    rec = causal_depthwise_conv(rec, conv_w, conv_b)
    rec = rg_lru(rec, w_a, b_a, w_x, b_x, lam)
    return (jax.nn.gelu(gate) * rec) @ w_out


def shared_kv(h, g, w_kv, w_f, b_f):
    Bn, S, _ = h.shape
    hn = rms_norm(h, g)
    kv = (hn @ w_kv).reshape(Bn, S, 2, N_HEADS, HEAD_DIM)
    k = kv[:, :, 0].transpose(0, 2, 1, 3)
    v = kv[:, :, 1].transpose(0, 2, 1, 3)
    log_f = jax.nn.log_sigmoid((hn @ w_f + b_f).astype(jnp.float32))
    c = jnp.cumsum(log_f, axis=1).transpose(0, 2, 1)
    return k, v, c


def forgetting_attention(xq, w_q, w_o, k, v, c):
    Bn, S, _ = xq.shape
    q = (xq @ w_q).reshape(Bn, S, N_HEADS, HEAD_DIM).transpose(0, 2, 1, 3) * (HEAD_DIM ** -0.5)
    outs = []
    for blk in range(S // Q_BLOCK):
        q0 = blk * Q_BLOCK
        end = q0 + Q_BLOCK
        logits = jnp.einsum('bhqd,bhkd->bhqk', q[:, :, q0:end], k[:, :, :end]).astype(jnp.float32)
        logits = logits + c[:, :, q0:end, None] - c[:, :, None, :end]
        qpos = q0 + jnp.arange(Q_BLOCK)
        kpos = jnp.arange(end)
        logits = jnp.where(kpos[None, :] <= qpos[:, None], logits, -jnp.inf)
        p = jax.nn.softmax(logits, axis=-1).astype(v.dtype)
        outs.append(jnp.einsum('bhqk,bhkd->bhqd', p, v[:, :, :end]))
    o = jnp.concatenate(outs, axis=2).transpose(0, 2, 1, 3).reshape(Bn, S, D_ATTN)
    return o @ w_o


def setup_inputs(seed: int = 0) -> dict:
    key = jax.random.key(seed)
    ks = iter(jax.random.split(key, 40))
    f32 = jnp.float32

    def nrm(shape, fan_in):
        return jax.random.normal(next(ks), shape, f32) * (fan_in ** -0.5)

    def gain(shape):
        return 1.0 + 0.05 * jax.random.normal(next(ks), shape, f32)

    def small(shape):
        return 0.02 * jax.random.normal(next(ks), shape, f32)

    L, NA, NB = DEPTH, N_A_LAYERS, N_B_LAYERS
    x = jax.random.normal(next(ks), (BATCH, SEQ, D_MODEL), f32)
    u = jax.random.uniform(next(ks), (NA, D_RNN), f32, minval=0.9, maxval=0.999)
    a0 = u ** (1.0 / LRU_C)
    rg_lambda = jnp.log(a0) - jnp.log1p(-a0)
    return {
        "x": x,
        "ffn1_pre_g": gain((L, D_MODEL)),
        "ffn1_w_gate": nrm((L, D_MODEL, D_FF), D_MODEL),
        "ffn1_w_up": nrm((L, D_MODEL, D_FF), D_MODEL),
        "ffn1_w_down": nrm((L, D_FF, D_MODEL), D_FF),
        "ffn1_post_g": gain((L, D_MODEL)),
        "mix_pre_g": gain((L, D_MODEL)),
        "mix_post_g": gain((L, D_MODEL)),
        "ffn2_pre_g": gain((L, D_MODEL)),
        "ffn2_w_gate": nrm((L, D_MODEL, D_FF), D_MODEL),
        "ffn2_w_up": nrm((L, D_MODEL, D_FF), D_MODEL),
        "ffn2_w_down": nrm((L, D_FF, D_MODEL), D_FF),
        "ffn2_post_g": gain((L, D_MODEL)),
        "rg_w_in": nrm((NA, D_MODEL, 2 * D_RNN), D_MODEL),
        "rg_conv_w": nrm((NA, CONV_WIDTH, D_RNN), CONV_WIDTH),
        "rg_conv_b": small((NA, D_RNN)),
        "rg_w_a": nrm((NA, N_LRU_BLOCKS, LRU_BLOCK, LRU_BLOCK), LRU_BLOCK),
        "rg_b_a": small((NA, D_RNN)),
        "rg_w_x": nrm((NA, N_LRU_BLOCKS, LRU_BLOCK, LRU_BLOCK), LRU_BLOCK),
        "rg_b_x": small((NA, D_RNN)),
        "rg_lambda": rg_lambda,
        "rg_w_out": nrm((NA, D_RNN, D_MODEL), D_RNN),
        "kv_norm_g": gain((D_MODEL,)),
        "w_kv": nrm((D_MODEL, 2 * D_ATTN), D_MODEL),
        "w_fgate": nrm((D_MODEL, N_HEADS), D_MODEL),
        "b_fgate": jax.random.uniform(next(ks), (N_HEADS,), f32, minval=1.0, maxval=4.0),
        "attn_w_q": nrm((NB, D_MODEL, D_ATTN), D_MODEL),
        "attn_w_o": nrm((NB, D_ATTN, D_MODEL), D_ATTN),
    }


def reference(x, ffn1_pre_g, ffn1_w_gate, ffn1_w_up, ffn1_w_down, ffn1_post_g,
              mix_pre_g, mix_post_g,
              ffn2_pre_g, ffn2_w_gate, ffn2_w_up, ffn2_w_down, ffn2_post_g,
              rg_w_in, rg_conv_w, rg_conv_b, rg_w_a, rg_b_a, rg_w_x, rg_b_x, rg_lambda, rg_w_out,
              kv_norm_g, w_kv, w_fgate, b_fgate, attn_w_q, attn_w_o):
    h = x
    k = v = c = None
    for layer in range(DEPTH):
        if layer == N_A_LAYERS:
            k, v, c = shared_kv(h, kv_norm_g, w_kv, w_fgate, b_fgate)
        f = swiglu(rms_norm(h, ffn1_pre_g[layer]), ffn1_w_gate[layer], ffn1_w_up[layer], ffn1_w_down[layer])
        h = h + 0.5 * rms_norm(f, ffn1_post_g[layer])
        hn = rms_norm(h, mix_pre_g[layer])
        if layer < N_A_LAYERS:
            j = layer
            m = recurrent_block(hn, rg_w_in[j], rg_conv_w[j], rg_conv_b[j], rg_w_a[j], rg_b_a[j],
                                rg_w_x[j], rg_b_x[j], rg_lambda[j], rg_w_out[j])
        else:
            j = layer - N_A_LAYERS
            m = forgetting_attention(hn, attn_w_q[j], attn_w_o[j], k, v, c)
        h = h + rms_norm(m, mix_post_g[layer])
        f = swiglu(rms_norm(h, ffn2_pre_g[layer]), ffn2_w_gate[layer], ffn2_w_up[layer], ffn2_w_down[layer])
        h = h + 0.5 * rms_norm(f, ffn2_post_g[layer])
    return h
```

```python
import contextlib
import numpy as np
import concourse.bass as bass
import concourse.mybir as mybir
from concourse.bass_utils import run_bass_kernel_spmd

F32 = mybir.dt.float32
BF16 = mybir.dt.bfloat16
AF = mybir.ActivationFunctionType
ALU = mybir.AluOpType

PE, ACT, DVE, POOL, SP = "pe", "act", "dve", "pool", "sp"
ENGS = (PE, ACT, DVE, POOL, SP)


class Buf:
    __slots__ = ("name", "lw", "rd")

    def __init__(self, name=""):
        self.name = name
        self.lw = None
        self.rd = []


class Op:
    __slots__ = ("eng", "fn", "waits", "is_dma", "dsem", "dval", "target", "semval")


class Prog:
    def __init__(self, n_dma_sems=32, n_conv_sems=2):
        self.ops = {e: [] for e in ENGS}
        self.n_dma_sems = n_dma_sems
        self.pools = {"main": list(range(0, n_dma_sems - n_conv_sems)),
                      "conv": list(range(n_dma_sems - n_conv_sems, n_dma_sems))}
        self.pool_rr = {"main": 0, "conv": 0}
        self.dma_sem_count = [0] * n_dma_sems
        self.dma_sem_last = [None] * n_dma_sems

    def _mk(self, eng, fn, reads, writes, is_dma, dsem=None):
        op = Op()
        op.eng = eng
        op.fn = fn
        op.is_dma = is_dma
        op.target = False
        op.semval = None
        op.dsem = dsem
        op.dval = None
        deps = []
        for b in reads:
            if b.lw is not None:
                deps.append(("raw", b.lw))
        for b in writes:
            if b.lw is not None:
                deps.append(("waw", b.lw))
            for r in b.rd:
                deps.append(("war", r))
        waits = []
        for kind, src in deps:
            sop = src[1]
            if (not sop.is_dma) and sop.eng == eng and not is_dma:
                if kind != "raw" or eng == PE:
                    continue
            if sop not in waits:
                waits.append(sop)
        op.waits = waits
        self.ops[eng].append(op)
        me = ("d" if is_dma else "e", op)
        for b in reads:
            if is_dma:
                b.rd = [r for r in b.rd if not (r[0] == "d" and r[1].dsem == op.dsem)]
            else:
                b.rd = [r for r in b.rd if not (r[0] == "e" and r[1].eng == eng)]
            b.rd.append(me)
        for b in writes:
            b.lw = me
            b.rd = []
        return op

    def op(self, eng, fn, reads=(), writes=()):
        return self._mk(eng, fn, list(reads), list(writes), False)

    def dma(self, eng, fn, reads=(), writes=(), pool="main"):
        lst = self.pools[pool]
        k = lst[self.pool_rr[pool] % len(lst)]
        self.pool_rr[pool] += 1
        op = self._mk(eng, fn, list(reads), list(writes), True, dsem=k)
        prev = self.dma_sem_last[k]
        if prev is not None and prev not in op.waits:
            op.waits.append(prev)
        self.dma_sem_count[k] += 16
        op.dval = self.dma_sem_count[k]
        self.dma_sem_last[k] = op
        return op


def _emit_engine(prog, e, h, esems, dsems):
    known = {}
    for op in prog.ops[e]:
        need = {}
        for w in op.waits:
            if w.is_dma:
                key, val = ("d", w.dsem), w.dval
            else:
                key, val = ("e", w.eng), w.semval
            if known.get(key, 0) >= val:
                continue
            if need.get(key, 0) < val:
                need[key] = val
        for key, val in need.items():
            sem = dsems[key[1]] if key[0] == "d" else esems[key[1]]
            h.wait_ge(sem, val)
            known[key] = val
        ins = op.fn(h)
        if op.is_dma:
            ins.then_inc(dsems[op.dsem], 16)
        elif op.target:
            ins.then_inc(esems[e], 1)


def run_prog(nc, prog, final_dma_ops):
    for e in ENGS:
        for op in prog.ops[e]:
            for w in op.waits:
                if not w.is_dma:
                    w.target = True
    for e in ENGS:
        c = 0
        for op in prog.ops[e]:
            if op.target and not op.is_dma:
                c += 1
                op.semval = c
    with contextlib.ExitStack() as st:
        esems = {e: st.enter_context(nc.semaphore("s_" + e)) for e in ENGS}
        dsems = [st.enter_context(nc.semaphore("d%d" % i)) for i in range(prog.n_dma_sems)]
        block = st.enter_context(nc.Block())

        def mk(e):
            def body(h):
                _emit_engine(prog, e, h, esems, dsems)
                if e == SP:
                    best = {}
                    for op in final_dma_ops:
                        best[op.dsem] = max(best.get(op.dsem, 0), op.dval)
                    for k, v in best.items():
                        h.wait_ge(dsems[k], v)
            return body

        block.tensor(mk(PE))
        block.scalar(mk(ACT))
        block.vector(mk(DVE))
        block.gpsimd(mk(POOL))
        block.sync(mk(SP))


D = 1024
DFF = 2816
DR = 1344
NB = 16
LB = 84
NH = 16
HD = 64
SEQ = 4096
T = 512
KC = 8
FC = 22
EPS = 1e-6
SLOT = 4096
NSLOT = 6
AUG = 70

WEIGHT_NAMES = [
    "ffn1_pre_g", "ffn1_w_gate", "ffn1_w_up", "ffn1_w_down", "ffn1_post_g", "mix_pre_g", "mix_post_g",
    "ffn2_pre_g", "ffn2_w_gate", "ffn2_w_up", "ffn2_w_down", "ffn2_post_g",
    "rg_w_in", "rg_conv_w", "rg_conv_b", "rg_w_a", "rg_b_a", "rg_w_x", "rg_b_x", "rg_lambda", "rg_w_out",
    "kv_norm_g", "w_kv", "w_fgate", "b_fgate", "attn_w_q", "attn_w_o",
]
WEIGHT_SHAPES = {
    "ffn1_pre_g": [2, D], "ffn1_w_gate": [2, D, DFF], "ffn1_w_up": [2, D, DFF], "ffn1_w_down": [2, DFF, D],
    "ffn1_post_g": [2, D], "mix_pre_g": [2, D], "mix_post_g": [2, D], "ffn2_pre_g": [2, D],
    "ffn2_w_gate": [2, D, DFF], "ffn2_w_up": [2, D, DFF], "ffn2_w_down": [2, DFF, D], "ffn2_post_g": [2, D],
    "rg_w_in": [1, D, 2 * DR], "rg_conv_w": [1, 4, DR], "rg_conv_b": [1, DR], "rg_w_a": [1, NB, LB, LB],
    "rg_b_a": [1, DR], "rg_w_x": [1, NB, LB, LB], "rg_b_x": [1, DR], "rg_lambda": [1, DR],
    "rg_w_out": [1, DR, D], "kv_norm_g": [D], "w_kv": [D, 2 * D], "w_fgate": [D, NH], "b_fgate": [NH],
    "attn_w_q": [1, D, D], "attn_w_o": [1, D, D],
}
GI = {("ffn1_pre", 0): 0, ("ffn1_post", 0): 1, ("mix_pre", 0): 2, ("mix_post", 0): 3, ("ffn2_pre", 0): 4,
      ("ffn2_post", 0): 5, ("kv", 0): 6, ("ffn1_pre", 1): 7, ("ffn1_post", 1): 8, ("mix_pre", 1): 9,
      ("mix_post", 1): 10, ("ffn2_pre", 1): 11, ("ffn2_post", 1): 12}


def make_consts():
    ident = np.eye(128, dtype=np.float32)
    tri = (np.arange(128)[:, None] <= np.arange(128)[None, :]).astype(np.float32)
    selk = np.zeros((128, NH, 6), np.float32)
    selq = np.zeros((128, NH, 6), np.float32)
    for h in range(NH):
        for j in range(3):
            selk[32 * j + h, h, j] = 1.0
            selk[96, h, 3 + j] = 1.0
            selq[96, h, j] = 8.0
            selq[32 * j + h, h, 3 + j] = -8.0
    return {"c_ident": ident, "c_tri": tri, "c_selk": selk.reshape(128, NH * 6), "c_selq": selq.reshape(128, NH * 6)}


def build(ntiles=SEQ // T, dbg_scr=False):
    nc = bass.Bass("TRN2", target_bir_lowering=False)
    P = Prog()
    st = contextlib.ExitStack()
    NT = ntiles

    def dram_in(name, shape):
        return nc.dram_tensor(name, list(shape), F32, kind="ExternalInput").ap()

    x_d = dram_in("x", [SEQ, D])
    W = {n: dram_in(n, WEIGHT_SHAPES[n]) for n in WEIGHT_NAMES}
    c_ident = dram_in("c_ident", [128, 128])
    c_tri = dram_in("c_tri", [128, 128])
    c_selk = dram_in("c_selk", [128, NH * 6])
    c_selq = dram_in("c_selq", [128, NH * 6])
    out_d = nc.dram_tensor("out", [SEQ, D], F32, kind="ExternalOutput").ap()

    def scratch(name, shape):
        return nc.dram_tensor(name, list(shape), BF16, kind="ExternalOutput" if dbg_scr else "Internal").ap()

    scr_gu = scratch("scr_gu", [4, 11, 128, 4096])
    scr_dn = scratch("scr_dn", [4, 4, 2, 128, 2816])
    scr_ri = scratch("scr_ri", [8, 128, 2688])
    scr_ga = scratch("scr_ga", [LB, 2688])
    scr_ro = scratch("scr_ro", [4, LB, 4096])
    scr_wk = scratch("scr_wk", [4, 128, 2048])
    scr_wv = scratch("scr_wv", [2, 128, 4096])
    scr_wf = scratch("scr_wf", [128, 128])
    scr_wq = scratch("scr_wq", [4, 128, 2048])
    scr_wo = scratch("scr_wo", [4, HD, 4096])
    kt_scr = scratch("kt_scr", [NH, AUG, SEQ])
    v_scr = scratch("v_scr", [NH, 128, 32 * 65])

    def sb(name, shape, dt):
        return st.enter_context(nc.sbuf_tensor(name, list(shape), dt))

    xT = sb("xT", [128, KC, T], F32)
    xn = sb("xn", [128, KC, T], BF16)
    A = sb("A", [128, FC, T], BF16)
    fT = sb("fT", [128, KC, T], F32)
    xin = sb("xin", [128, 4, D], F32)
    sq = sb("sq", [128, 3, T], BF16)
    rstd = sb("rstd", [128, T], F32)
    sg = sb("sg", [128, 2, T], F32)
    ring = sb("ring", [128, NSLOT, SLOT], BF16)
    ident = sb("ident", [128, 128], F32)
    ones_bf = sb("ones_bf", [128, 128], BF16)
    ones_f = sb("ones_f", [128, 64], F32)
    ones16 = sb("ones16", [16, T], F32)
    tri = sb("tri", [128, 128], BF16)
    selk = sb("selk", [128, NH * 6], BF16)
    selq = sb("selq", [128, NH * 6], BF16)
    gstage = sb("gstage", [104, 128], F32)
    G = sb("G", [128, 104], F32)
    Gh = sb("Gh", [128, 104], F32)
    rvstage = sb("rvstage", [128, LB], F32)
    RV = sb("RV", [LB, 128], F32)
    RD = sb("RD", [LB, 64], F32)
    nbf = sb("nbf", [16, 2], F32)
    convst = sb("convst", [LB, NB, 3], F32)
    hstate = sb("hstate", [LB, NB], F32)
    cstate = sb("cstate", [16, 1], F32)
    xr2 = sb("xr", [LB, 2, T + 3], F32)
    ycv2 = sb("ycv", [LB, 2, T], F32)
    ybf2 = sb("ybf", [LB, 2, T], BF16)
    b12 = sb("b1", [LB, 2, T], F32)
    b22 = sb("b2", [LB, 2, T], F32)
    b32 = sb("b3", [LB, 2, T], F32)
    b42 = sb("b4", [LB, 2, T], F32)
    hh2 = sb("hh_t", [LB, 2, T], F32)
    g2b = sb("g_sb", [LB, 2, T], F32)
    b1, b2, b3, ybf = b12[:, 0, :], b22[:, 0, :], b32[:, 0, :], ybf2[:, 0, :]
    csplit = sb("csplit", [128, T], BF16)
    KTcur = sb("KTcur", [AUG, NH, T], BF16)
    Vcur = sb("Vcur", [128, NH, 4, 65], BF16)
    QTv = fT[:, :, :].bitcast(BF16)

    def QT(h_):
        return QTv[0:AUG, h_ // 2, (h_ % 2) * T:(h_ % 2 + 1) * T]

    ps = [st.enter_context(nc.psum_tensor("ps%d" % i, [128, T], F32)) for i in range(8)]

    b_xT = [Buf("xT%d" % k) for k in range(KC)]
    b_xn = [Buf("xn%d" % k) for k in range(KC)]
    b_A = [Buf("A%d" % c) for c in range(FC)]
    b_fT = [Buf("fT%d" % c) for c in range(KC)]
    b_xin = [Buf("xin%d" % b) for b in range(4)]
    b_sq = [Buf("sq%d" % i) for i in range(3)]
    b_rstd = Buf("rstd")
    b_sg = [Buf("sg0"), Buf("sg1")]
    b_slot = [Buf("slot%d" % s) for s in range(NSLOT)]
    b_ps = [Buf("ps%d" % i) for i in range(8)]
    b_const = Buf("const")
    b_G, b_RV, b_RD, b_nbf = Buf("G"), Buf("RV"), Buf("RD"), Buf("nbf")
    b_gstage, b_rvstage = Buf("gstage"), Buf("rvstage")
    b_convst = [Buf("convst%d" % j) for j in range(NB)]
    b_hstate = [Buf("hstate%d" % j) for j in range(NB)]
    b_cstate = Buf("cstate")
    bt2 = [{n: Buf(n + str(p)) for n in ["xr", "ycv", "ybf", "b1", "b2", "b3", "b4", "hh", "g"]} for p in range(2)]
    bt = dict(bt2[0])
    bt["csplit"] = Buf("csplit")
    b_KT = [Buf("KT%d" % h) for h in range(NH)]
    b_QT = [b_fT[h // 2] for h in range(NH)]
    b_V = [Buf("V%d" % tb) for tb in range(4)]
    b_PT = [b_A[16 + i] for i in range(4)]
    b_ktscr = [Buf("ktscr%d" % i) for i in range(NT)]
    b_vscr = [Buf("vscr%d" % i) for i in range(NT)]
    b_out = Buf("out")

    def act(out, in_, func, reads, writes, bias=None, scale=None):
        kw = {}
        if bias is not None:
            kw["bias"] = bias
        if scale is not None:
            kw["scale"] = scale
        return P.op(ACT, lambda h: h.activation(out=out, in_=in_, func=func, **kw), reads, writes)

    def tt(eng, out, in0, in1, op, reads, writes):
        return P.op(eng, lambda h: h.tensor_tensor(out=out, in0=in0, in1=in1, op=op), reads, writes)

    def ts(eng, out, in0, s1, s2, op0, op1, reads, writes):
        return P.op(eng, lambda h: h.tensor_scalar(out=out, in0=in0, scalar1=s1, scalar2=s2, op0=op0, op1=op1),
                    reads, writes)

    def stt(out, in0, scalar, in1, op0, op1, reads, writes):
        return P.op(DVE, lambda h: h.scalar_tensor_tensor(out=out, in0=in0, scalar=scalar, in1=in1, op0=op0, op1=op1),
                    reads, writes)

    def cp(eng, out, in_, reads, writes):
        return P.op(eng, lambda h: h.tensor_copy(out=out, in_=in_), reads, writes)

    def mm(out, lhsT, rhs, start, stop, reads, writes, tile_position=None):
        if tile_position is None:
            return P.op(PE, lambda h: h.matmul(out, lhsT=lhsT, rhs=rhs, start=start, stop=stop), reads, writes)
        return P.op(PE, lambda h: h.matmul(out, lhsT=lhsT, rhs=rhs, start=start, stop=stop,
                                           tile_position=tile_position), reads, writes)

    b_scr = {}

    def kp(w2d):
        return w2d.rearrange("(k p) c -> p k c", p=128)

    def ffn_w(f):
        l = f // 2
        pre = "ffn1" if f % 2 == 0 else "ffn2"
        return kp(W[pre + "_w_gate"][l]), kp(W[pre + "_w_up"][l]), kp(W[pre + "_w_down"][l])

    def piece(key):
        kind = key[0]
        if kind == "gu":
            _, f, j = key
            wg, wu, _ = ffn_w(f)
            return scr_gu[f, j], 128, 4096, [(0, wg[:, :, j * 256:(j + 1) * 256]), (2048, wu[:, :, j * 256:(j + 1) * 256])]
        if kind == "dn":
            _, f, q, hf = key
            wd = ffn_w(f)[2]
            return scr_dn[f, q, hf], 128, 2816, [(0, wd[:, hf * 11:(hf + 1) * 11, q * 256:(q + 1) * 256])]
        if kind == "ri":
            p8 = key[1]
            wi = kp(W["rg_w_in"][0])
            return scr_ri[p8], 128, 2688, [(0, wi[:, :, p8 * 168:(p8 + 1) * 168]),
                                           (1344, wi[:, :, DR + p8 * 168:DR + (p8 + 1) * 168])]
        if kind == "ga":
            return scr_ga, LB, 2688, [(0, W["rg_w_a"][0].rearrange("n c d -> c n d")),
                                      (1344, W["rg_w_x"][0].rearrange("n c d -> c n d"))]
        if kind == "ro":
            q = key[1]
            wo = W["rg_w_out"][0].rearrange("(n c) o -> c n o", c=LB)
            return scr_ro[q], LB, 4096, [(0, wo[:, :, q * 256:(q + 1) * 256])]
        if kind == "wk":
            q = key[1]
            return scr_wk[q], 128, 2048, [(0, kp(W["w_kv"])[:, :, q * 256:(q + 1) * 256])]
        if kind == "wv":
            hf = key[1]
            return scr_wv[hf], 128, 4096, [(0, kp(W["w_kv"])[:, :, D + hf * 512:D + (hf + 1) * 512])]
        if kind == "wf":
            return scr_wf, 128, 128, [(0, kp(W["w_fgate"]))]
        if kind == "wq":
            q = key[1]
            return scr_wq[q], 128, 2048, [(0, kp(W["attn_w_q"][0])[:, :, q * 256:(q + 1) * 256])]
        if kind == "wo":
            q = key[1]
            wo = W["attn_w_o"][0].rearrange("(h d) o -> d h o", d=HD)
            return scr_wo[q], HD, 4096, [(0, wo[:, :, q * 256:(q + 1) * 256])]
        raise KeyError(key)

    def init():
        P.dma(SP, lambda h: h.dma_start(out=ident[:], in_=c_ident), writes=[b_const])
        P.dma(SP, lambda h: h.dma_start(out=xin[:, 3, 0:128], in_=c_tri), writes=[b_xin[3]])
        P.dma(SP, lambda h: h.dma_start(out=xin[:, 3, 128:224], in_=c_selk), writes=[b_xin[3]])
        P.dma(SP, lambda h: h.dma_start(out=xin[:, 3, 224:320], in_=c_selq), writes=[b_xin[3]])
        cp(DVE, tri[:], xin[:, 3, 0:128], [b_xin[3]], [b_const])
        cp(DVE, selk[:], xin[:, 3, 128:224], [b_xin[3]], [b_const])
        cp(DVE, selq[:], xin[:, 3, 224:320], [b_xin[3]], [b_const])
        P.op(DVE, lambda h: h.memset(ones_bf[:], 1.0), writes=[b_const])
        P.op(DVE, lambda h: h.memset(ones_f[:], 1.0), writes=[b_const])
        P.op(DVE, lambda h: h.memset(ones16[:], 1.0), writes=[b_const])
        P.op(DVE, lambda h: h.memset(csplit[:], 1.0), writes=[bt["csplit"]])
        P.op(DVE, lambda h: h.memset(Vcur[:], 1.0), writes=b_V)
        P.op(DVE, lambda h: h.memset(convst[:], 0.0), writes=b_convst)
        P.op(DVE, lambda h: h.memset(hstate[:], 0.0), writes=b_hstate)
        P.op(DVE, lambda h: h.memset(cstate[:], 0.0), writes=[b_cstate])
        gsrc = {}
        for (nm, l), gi in GI.items():
            gsrc[gi] = W["kv_norm_g"] if nm == "kv" else W[nm + "_g"][l]
        for gi in range(13):
            P.dma(SP, lambda h, gi=gi: h.dma_start(out=gstage[gi * 8:(gi + 1) * 8, :],
                                                   in_=gsrc[gi].rearrange("(k p) -> k p", p=128)),
                  writes=[b_gstage])
        P.op(PE, lambda h: h.transpose(ps[0][:, 0:104], gstage[:, :], ident[0:104, 0:104]),
             reads=[b_gstage, b_const], writes=[b_ps[0]])
        cp(DVE, G[:], ps[0][:, 0:104], [b_ps[0]], [b_G])
        P.op(DVE, lambda h: h.tensor_scalar_mul(out=Gh[:], in0=G[:], scalar1=0.5), [b_G], [b_G])
        vsrc = [W["rg_conv_w"][0, 0], W["rg_conv_w"][0, 1], W["rg_conv_w"][0, 2], W["rg_conv_w"][0, 3],
                W["rg_conv_b"][0], W["rg_b_a"][0], W["rg_b_x"][0], W["rg_lambda"][0]]
        for v in range(8):
            P.dma(SP, lambda h, v=v: h.dma_start(out=rvstage[v * 16:(v + 1) * 16, :],
                                                 in_=vsrc[v].rearrange("(n c) -> n c", c=LB)),
                  writes=[b_rvstage])
        P.op(PE, lambda h: h.transpose(ps[1][0:LB, 0:128], rvstage[:, :], ident[:, :]),
             reads=[b_rvstage, b_const], writes=[b_ps[1]])
        cp(DVE, RV[:], ps[1][0:LB, 0:128], [b_ps[1]], [b_RV])
        P.op(DVE, lambda h: h.tensor_scalar_mul(out=RD[:, 0:32], in0=RV[:, 80:112], scalar1=0.5), [b_RV], [b_RD])
        act(RD[:, 48:64], RV[:, 112:128], AF.Exp, [b_RV], [b_RD], scale=-1.0)
        act(RD[:, 48:64], RD[:, 48:64], AF.Ln, [b_RD], [b_RD], bias=1.0)
        P.op(DVE, lambda h: h.tensor_scalar_mul(out=RD[:, 32:48], in0=RD[:, 48:64], scalar1=-4.0), [b_RD], [b_RD])
        P.dma(SP, lambda h: h.dma_start(out=nbf[:, 0:1], in_=W["b_fgate"].rearrange("(h o) -> h o", o=1)),
              writes=[b_nbf])
        P.op(DVE, lambda h: h.tensor_scalar_mul(out=nbf[:, 1:2], in0=nbf[:, 0:1], scalar1=-1.0), [b_nbf], [b_nbf])

    cur = {"pl": POOL, "tile": 0}

    def PL():
        return cur["pl"]

    ring_state = {"n": 0}

    def next_slot():
        s_ = ring_state["n"] % NSLOT
        ring_state["n"] += 1
        return s_

    def load_piece(src_ap, npart, nelem, src_bufs):
        s_ = next_slot()
        P.dma(SP, lambda h: h.dma_start(out=ring[0:npart, s_, 0:nelem], in_=src_ap), reads=src_bufs,
              writes=[b_slot[s_]])
        return s_

    cast_rr = {"n": 0, "stg": 0}

    def load_w(key):
        scr_ap, npart, nelem, views = piece(key)
        b = b_scr.setdefault(key, Buf("scr" + str(key)))
        if cur["tile"] > 0:
            return load_piece(scr_ap, npart, nelem, [b])
        s_ = next_slot()
        for off, src in views:
            nrow, ninner = src.shape[1], src.shape[2]
            step = max(1, 1024 // ninner)
            for r0 in range(0, nrow, step):
                rows = min(step, nrow - r0)
                n = rows * ninner
                sgi = cast_rr["stg"] % 4
                cast_rr["stg"] += 1
                stg = xin[0:npart, sgi, 0:n]
                P.dma(SP, lambda h, stg=stg, src=src, r0=r0, rows=rows, ninner=ninner: h.dma_start(
                    out=stg.rearrange("p (r c) -> p r c", c=ninner), in_=src[:, r0:r0 + rows, :]),
                    reads=[], writes=[b_xin[sgi]])
                dst = ring[0:npart, s_, off + r0 * ninner:off + r0 * ninner + n]
                eng = (POOL, ACT, DVE)[cast_rr["n"] % 3]
                cast_rr["n"] += 1
                if eng == ACT:
                    act(dst, stg, AF.Copy, [b_xin[sgi]], [b_slot[s_]])
                else:
                    cp(eng, dst, stg, [b_xin[sgi]], [b_slot[s_]])
        P.dma(ACT, lambda h: h.dma_start(out=scr_ap, in_=ring[0:npart, s_, 0:nelem]), reads=[b_slot[s_]], writes=[b])
        return s_

    class Run:
        LA = 1

        def __init__(self, keys, loader=None):
            self.keys = list(keys)
            self.slots = {}
            self.n = 0
            self.loader = loader or load_w

        def get(self, idx):
            while self.n < len(self.keys) and self.n <= idx + self.LA:
                self.slots[self.n] = self.loader(self.keys[self.n])
                self.n += 1
            return self.slots[idx]

    sq_state = {"n": 0}
    nstate = {"acc": False, "rstd": False}

    def stats_sq(src, src_buf, eng=None):
        i = sq_state["n"] % 3
        sq_state["n"] += 1
        if eng == ACT:
            act(sq[:, i, :], src, AF.Square, [src_buf], [b_sq[i]])
        else:
            tt(PL(), sq[:, i, :], src, src, ALU.mult, [src_buf], [b_sq[i]])

        def acc(first, last):
            mm(ps[7][:, :], ones_bf[:, :], sq[:, i, :], first, last, [b_const, b_sq[i]], [b_ps[7]])
        return acc

    def stats_finish():
        act(rstd[:], ps[7][:, :], AF.Sqrt, [b_ps[7]], [b_rstd], bias=EPS, scale=1.0 / D)
        P.op(DVE, lambda h: h.reciprocal(out=rstd[:], in_=rstd[:]), [b_rstd], [b_rstd])

    def pre_norm(gi):
        if not nstate["rstd"]:
            if not nstate["acc"]:
                for k in range(KC):
                    stats_sq(xT[:, k, :], b_xT[k], ACT)(k == 0, k == KC - 1)
            stats_finish()
        nstate["acc"] = False
        nstate["rstd"] = True
        for k in range(KC):
            stt(xn[:, k, :], xT[:, k, :], G[:, gi * 8 + k:gi * 8 + k + 1], rstd[:], ALU.mult, ALU.mult,
                [b_xT[k], b_G, b_rstd], [b_xn[k]])

    pend_stats = []

    def post_chunk(c, bank, npart=128):
        act(fT[:, c, :], ps[bank][:, :], AF.Copy, [b_ps[bank]], [b_fT[c]])
        i = sq_state["n"] % 3
        sq_state["n"] += 1
        act(sq[:, i, :], ps[bank][:, :], AF.Square, [b_ps[bank]], [b_sq[i]])
        pend_stats.append(lambda: mm(ps[7][:, :], ones_bf[:, :], sq[:, i, :], c == 0, c == KC - 1,
                                     [b_const, b_sq[i]], [b_ps[7]]))

    def flush_stats(keep=0):
        while len(pend_stats) > keep:
            pend_stats.pop(0)()

    def post_norm(gi, half, next_pre=True):
        Gx = Gh if half else G
        flush_stats()
        stats_finish()
        for c in range(KC):
            tt(PL() if c % 2 == 0 else DVE, fT[:, c, :], fT[:, c, :], rstd[:], ALU.mult, [b_fT[c], b_rstd], [b_fT[c]])
            stt(xT[:, c, :], fT[:, c, :], Gx[:, gi * 8 + c:gi * 8 + c + 1], xT[:, c, :], ALU.mult, ALU.add,
                [b_fT[c], b_G, b_xT[c]], [b_xT[c]])
            if next_pre:
                stats_sq(xT[:, c, :], b_xT[c], ACT)(c == 0, c == KC - 1)
        nstate["acc"] = next_pre
        nstate["rstd"] = False

    def ffn(f):
        l = f // 2
        nm = "ffn1" if f % 2 == 0 else "ffn2"
        run = Run([("gu", f, j) for j in range(11)] + [("dn", f, q, hf) for q in range(4) for hf in range(2)])
        run.get(0)
        pre_norm(GI[(nm + "_pre", l)])
        for j in range(11):
            s = run.get(j)
            if j == 0:
                for k in range(KC):
                    for hc in range(2):
                        for a in range(2):
                            bank = 2 * hc + a
                            off = (a * KC + k) * 256 + hc * 128
                            mm(ps[bank][:, :], ring[:, s, off:off + 128], xn[:, k, :], k == 0, k == KC - 1,
                               [b_slot[s], b_xn[k]], [b_ps[bank]])
            for hc in range(2):
                m = 2 * j + hc
                bg, bu = (0, 1) if m % 2 == 0 else (2, 3)
                for a, bank in ((0, bg), (1, bu)):
                    for k in range(KC):
                        if j == 0:
                            break
                        off = (a * KC + k) * 256 + hc * 128
                        mm(ps[bank][:, :], ring[:, s, off:off + 128], xn[:, k, :], k == 0, k == KC - 1,
                           [b_slot[s], b_xn[k]], [b_ps[bank]])
                act(sg[:, m % 2, :], ps[bg][:, :], AF.Silu, [b_ps[bg]], [b_sg[m % 2]])
                tt(DVE, A[:, m, :], sg[:, m % 2, :], ps[bu][:, :], ALU.mult, [b_sg[m % 2], b_ps[bu]], [b_A[m]])
        for q in range(4):
            sh = [run.get(11 + 2 * q), run.get(11 + 2 * q + 1)]
            for cc in range(2):
                c = 2 * q + cc
                bank = 4 + c % 3
                for k in range(FC):
                    s = sh[k // 11]
                    off = (k % 11) * 256 + cc * 128
                    mm(ps[bank][:, :], ring[:, s, off:off + 128], A[:, k, :], k == 0, k == FC - 1,
                       [b_slot[s], b_A[k]], [b_ps[bank]])
                post_chunk(c, bank)
                flush_stats(keep=1)
        post_norm(GI[(nm + "_post", l)], True, next_pre=(f != 3))

    def rg():
        def ga_again(key):
            return load_piece(scr_ga, LB, 2688, [b_scr[("ga",)]])
        run = Run([("ga",)] + [("ri", p) for p in range(4)] + [("ga2",)] + [("ri", p) for p in range(4, 8)],
                  loader=lambda key: ga_again(key) if key[0] == "ga2" else load_w(key))
        run.get(0)
        pre_norm(GI[("mix_pre", 0)])
        ro_run = Run([("ro", q) for q in range(4)])

        def views(j):
            par = j % 2
            return (bt2[par], xr2[:, par, :], ycv2[:, par, :], ybf2[:, par, :], b12[:, par, :], b22[:, par, :],
                    b32[:, par, :], b42[:, par, :], hh2[:, par, :], g2b[:, par, :])

        def phase_a(j):
            btp, xr, ycv, ybf_, b1_, b2_, b3_, b4_, hh_t, g_sb = views(j)
            p8, jj = j // 2, j % 2
            s = run.get(1 + p8 if p8 < 4 else 2 + p8)
            sga = run.get(0 if p8 < 4 else 5)
            bgate, brec = (0, 1) if j % 2 == 0 else (2, 3)
            for a_, bank in ((0, bgate), (1, brec)):
                for k in range(KC):
                    off = (a_ * KC + k) * 168 + jj * LB
                    mm(ps[bank][0:LB, :], ring[:, s, off:off + LB], xn[:, k, :], k == 0, k == KC - 1,
                       [b_slot[s], b_xn[k]], [b_ps[bank]])
            cp(PL(), xr[:, 0:3], convst[:, j, :], [b_convst[j]], [btp["xr"]])
            act(xr[:, 3:3 + T], ps[brec][0:LB, :], AF.Copy, [b_ps[brec]], [btp["xr"]])
            act(g_sb, ps[bgate][0:LB, :], AF.Copy, [b_ps[bgate]], [btp["g"]])
            cp(PL(), convst[:, j, :], xr[:, T:T + 3], [btp["xr"]], [b_convst[j]])
            ts(DVE, ycv, xr[:, 0:T], RV[:, 0 * 16 + j:0 * 16 + j + 1], RV[:, 4 * 16 + j:4 * 16 + j + 1],
               ALU.mult, ALU.add, [btp["xr"], b_RV], [btp["ycv"]])
            for tap in range(1, 4):
                stt(ycv, xr[:, tap:tap + T], RV[:, tap * 16 + j:tap * 16 + j + 1], ycv, ALU.mult, ALU.add,
                    [btp["xr"], b_RV, btp["ycv"]], [btp["ycv"]])
            cp(PL(), ybf_, ycv, [btp["ycv"]], [btp["ybf"]])
            gr, gi_ = (4, 5) if j % 2 == 0 else (6, 7)
            mm(ps[gr][0:LB, :], ring[0:LB, sga, (0 * NB + j) * LB:(0 * NB + j + 1) * LB], ybf_, True, True,
               [b_slot[sga], btp["ybf"]], [b_ps[gr]])
            mm(ps[gi_][0:LB, :], ring[0:LB, sga, (1 * NB + j) * LB:(1 * NB + j + 1) * LB], ybf_, True, True,
               [b_slot[sga], btp["ybf"]], [b_ps[gi_]])
            tt(PL(), b4_, g_sb, g_sb, ALU.mult, [btp["g"]], [btp["b4"]])
            ts(PL(), b4_, b4_, 0.044715, 1.0, ALU.mult, ALU.add, [btp["b4"]], [btp["b4"]])
            tt(PL(), b4_, b4_, g_sb, ALU.mult, [btp["b4"], btp["g"]], [btp["b4"]])

        def phase_a2(j):
            btp, xr, ycv, ybf_, b1_, b2_, b3_, b4_, hh_t, g_sb = views(j)
            gr, gi_ = (4, 5) if j % 2 == 0 else (6, 7)
            act(b1_, ps[gr][0:LB, :], AF.Tanh, [b_ps[gr], b_RD], [btp["b1"]], bias=RD[:, j:j + 1], scale=0.5)
            act(b3_, ps[gi_][0:LB, :], AF.Tanh, [b_ps[gi_], b_RD], [btp["b3"]], bias=RD[:, 16 + j:17 + j], scale=0.5)

        def phase_b(j):
            btp, xr, ycv, ybf_, b1_, b2_, b3_, b4_, hh_t, g_sb = views(j)
            act(b1_, b1_, AF.Exp, [btp["b1"], b_RD], [btp["b1"]], bias=RD[:, 32 + j:33 + j], scale=RD[:, 32 + j:33 + j])
            act(b4_, b4_, AF.Tanh, [btp["b4"]], [btp["b4"]], scale=0.7978845608028654)
            act(b2_, b1_, AF.Square, [btp["b1"]], [btp["b2"]])
            act(b2_, b2_, AF.Sqrt, [btp["b2"]], [btp["b2"]], bias=0.25, scale=-0.25)
            stt(b3_, b3_, 1.0, ycv, ALU.add, ALU.mult, [btp["b3"], btp["ycv"]], [btp["b3"]])
            stt(b4_, b4_, 1.0, g_sb, ALU.add, ALU.mult, [btp["b4"], btp["g"]], [btp["b4"]])
            tt(PL(), b2_, b2_, b3_, ALU.mult, [btp["b2"], btp["b3"]], [btp["b2"]])
            P.op(DVE, lambda h, j=j, hh_t=hh_t, b1_=b1_, b2_=b2_: h.tensor_tensor_scan(
                out=hh_t, data0=b1_, data1=b2_, initial=hstate[:, j:j + 1], op0=ALU.mult, op1=ALU.add),
                [btp["b1"], btp["b2"], b_hstate[j]], [btp["hh"]])
            cp(PL(), hstate[:, j:j + 1], hh_t[:, T - 1:T], [btp["hh"]], [b_hstate[j]])
            stt(A[0:LB, j, :], b4_, 0.5, hh_t, ALU.mult, ALU.mult, [btp["b4"], btp["hh"]], [b_A[j]])

        phase_a(0)
        phase_a2(0)
        for j in range(NB):
            if j + 1 < NB:
                phase_a(j + 1)
            else:
                ro_run.get(0)
            phase_b(j)
            if j + 1 < NB:
                phase_a2(j + 1)
        for q in range(4):
            s = ro_run.get(q)
            for cc in range(2):
                c = 2 * q + cc
                bank = c % 2
                for n in range(NB):
                    off = n * 256 + cc * 128
                    mm(ps[bank][:, :], ring[0:LB, s, off:off + 128], A[0:LB, n, :], n == 0, n == NB - 1,
                       [b_slot[s], b_A[n]], [b_ps[bank]])
                post_chunk(c, bank)
                flush_stats(keep=1)
        post_norm(GI[("mix_post", 0)], False)

    def kv(i):
        run = Run([("wf",)] + [("wk", q) for q in range(4)] + [("wv", hf) for hf in range(2)])
        run.get(0)
        pre_norm(GI[("kv", 0)])
        s = run.get(0)
        for k in range(KC):
            mm(ps[4][0:16, :], ring[:, s, k * 16:(k + 1) * 16], xn[:, k, :], k == 0, k == KC - 1,
               [b_slot[s], b_xn[k]], [b_ps[4]])
        fl_e, fl_sp, cneg, c_hi, c_mid = b1[0:16, :], b2[0:16, :], b3[0:16, :], ybf[0:16, :], A[0:16, 21, :]
        c_r1, c_r2 = b1[0:16, :], b2[0:16, :]
        act(fl_e, ps[4][0:16, :], AF.Exp, [b_ps[4], b_nbf], [bt["b1"]], bias=nbf[:, 1:2], scale=-1.0)
        act(fl_sp, fl_e, AF.Ln, [bt["b1"]], [bt["b2"]], bias=1.0)
        P.op(DVE, lambda h: h.tensor_tensor_scan(out=cneg, data0=ones16[:], data1=fl_sp,
                                                 initial=cstate[:, 0:1], op0=ALU.mult, op1=ALU.add),
             [b_const, bt["b2"], b_cstate], [bt["b3"]])
        cp(PL(), cstate[:, 0:1], b3[0:16, T - 1:T], [bt["b3"]], [b_cstate])
        cp(DVE, c_hi, cneg, [bt["b3"]], [bt["ybf"]])
        tt(DVE, c_r1, cneg, c_hi, ALU.subtract, [bt["b3"], bt["ybf"]], [bt["b1"]])
        cp(DVE, c_mid, c_r1, [bt["b1"]], [b_A[21]])
        tt(DVE, c_r2, c_r1, c_mid, ALU.subtract, [bt["b1"], b_A[21]], [bt["b2"]])
        cp(PL(), csplit[0:16, :], c_hi, [bt["ybf"]], [bt["csplit"]])
        cp(PL(), csplit[32:48, :], c_mid, [b_A[21]], [bt["csplit"]])
        cp(DVE, csplit[64:80, :], c_r2, [bt["b2"]], [bt["csplit"]])
        for q in range(4):
            s = run.get(1 + q)
            for hh in range(4):
                h_ = 4 * q + hh
                bank = h_ % 2
                for k in range(KC):
                    off = k * 256 + hh * HD
                    mm(ps[bank][0:HD, :], ring[:, s, off:off + HD], xn[:, k, :], k == 0, k == KC - 1,
                       [b_slot[s], b_xn[k]], [b_ps[bank]])
                mm(ps[bank][HD:AUG, :], selk[:, h_ * 6:(h_ + 1) * 6], csplit[:, :], True, True,
                   [b_const, bt["csplit"]], [b_ps[bank]], tile_position=(0, HD))
                act(KTcur[:, h_, :], ps[bank][0:AUG, :], AF.Copy, [b_ps[bank]], [b_KT[h_]])
        for hf in range(2):
            s = run.get(5 + hf)
            for tb in range(4):
                bank = 2 + tb % 2
                for k in range(KC):
                    mm(ps[bank][:, :], xn[:, k, tb * 128:(tb + 1) * 128], ring[:, s, k * 512:(k + 1) * 512],
                       k == 0, k == KC - 1, [b_slot[s], b_xn[k]], [b_ps[bank]])
                cp(DVE, Vcur[:, hf * 8:(hf + 1) * 8, tb, 0:HD], ps[bank][:, :].rearrange("p (h d) -> p h d", h=8),
                   [b_ps[bank]], [b_V[tb]])
        if i < NT - 1:
            P.dma(ACT, lambda h: h.dma_start(out=kt_scr[:, :, i * T:(i + 1) * T].rearrange("h r c -> r h c"),
                                             in_=KTcur[:, :, :]), reads=b_KT, writes=[b_ktscr[i]])
            P.dma(ACT, lambda h: h.dma_start(
                out=v_scr[:, :, i * 260:(i + 1) * 260].rearrange("h p e -> p h e"),
                in_=Vcur[:, :, :, :].rearrange("p h b d -> p h (b d)")), reads=b_V, writes=[b_vscr[i]])

    st_rot = {"s": 0, "p": 0}

    def attn(i):
        qrun = Run([("wq", q) for q in range(4)])
        qrun.get(0)
        pre_norm(GI[("mix_pre", 1)])
        nprev = 4 * i
        sq_slot = None
        for h_ in range(NH):
            if h_ % 4 == 0:
                sq_slot = qrun.get(h_ // 4)
            hh = h_ % 4
            qb = h_ % 2
            for k in range(KC):
                off = k * 256 + hh * HD
                mm(ps[qb][0:HD, :], ring[:, sq_slot, off:off + HD], xn[:, k, :], k == 0, k == KC - 1,
                   [b_slot[sq_slot], b_xn[k]], [b_ps[qb]])
            mm(ps[qb][HD:AUG, :], selq[:, h_ * 6:(h_ + 1) * 6], csplit[:, :], True, True,
               [b_const, bt["csplit"]], [b_ps[qb]], tile_position=(0, HD))
            act(QT(h_), ps[qb][0:AUG, :], AF.Copy, [b_ps[qb]], [b_QT[h_]])
        LOOK = 2
        pend = []
        finals_q = []
        for h_ in range(NH):
            if i > 0:
                sk = load_piece(kt_scr[h_, :, 0:T * i], AUG, T * i, b_ktscr[0:i])
                sv = load_piece(v_scr[h_, :, 0:260 * i], 128, 260 * i, b_vscr[0:i])
            ob = 5 + h_ % 2
            nkb = nprev + 4

            def emit_s(kb):
                sbank = 2 + st_rot["s"] % 3
                st_rot["s"] += 1
                pi = st_rot["p"] % 4
                st_rot["p"] += 1
                if kb < nprev:
                    kt_ap, kt_b = ring[0:AUG, sk, kb * 128:(kb + 1) * 128], [b_slot[sk]]
                    v_ap, v_b = ring[:, sv, kb * 65:(kb + 1) * 65], [b_slot[sv]]
                    j = -1
                else:
                    j = kb - nprev
                    kt_ap, kt_b = KTcur[:, h_, j * 128:(j + 1) * 128], [b_KT[h_]]
                    v_ap, v_b = Vcur[:, h_, j, :], [b_V[j]]
                mm(ps[sbank][:, :], kt_ap, QT(h_), True, True, kt_b + [b_QT[h_]], [b_ps[sbank]])
                if j <= 0:
                    act(A[:, 16 + pi, :], ps[sbank][:, :], AF.Exp, [b_ps[sbank]], [b_PT[pi]], scale=0.125)
                else:
                    P.op(PL(), lambda h, pi=pi, j=j: h.memset(A[:, 16 + pi, 0:128 * j], 0.0), [], [b_PT[pi]])
                    act(A[:, 16 + pi, 128 * j:T], ps[sbank][:, 128 * j:T], AF.Exp, [b_ps[sbank]], [b_PT[pi]], scale=0.125)
                if j >= 0:
                    tt(DVE, A[:, 16 + pi, 128 * j:128 * (j + 1)], A[:, 16 + pi, 128 * j:128 * (j + 1)], tri[:, :],
                       ALU.mult, [b_PT[pi], b_const], [b_PT[pi]])
                return (kb, pi, v_ap, v_b, ob, nkb, h_)

            def emit_pv(item):
                kb, pi, v_ap, v_b, ob_, nkb_, hh_ = item
                mm(ps[ob_][0:65, :], v_ap, A[:, 16 + pi, :], kb == 0, kb == nkb_ - 1, v_b + [b_PT[pi]], [b_ps[ob_]])
                if kb == nkb_ - 1:
                    def finalize(hh_=hh_, ob_=ob_):
                        P.op(DVE, lambda h: h.reciprocal(out=b1[64:65, :], in_=ps[ob_][64:65, :]), [b_ps[ob_]],
                             [bt["b1"]])
                        mm(ps[7][0:HD, :], ones_f[64:65, 0:HD], b1[64:65, :], True, True, [b_const, bt["b1"]],
                           [b_ps[7]], tile_position=(64, 0))
                        act(b2[0:HD, :], ps[7][0:HD, :], AF.Copy, [b_ps[7]], [bt["b2"]])
                        tt(DVE, A[0:HD, hh_, :], ps[ob_][0:HD, :], b2[0:HD, :], ALU.mult, [b_ps[ob_], bt["b2"]],
                           [b_A[hh_]])
                    finals_q.append([finalize, 2])

            for kb in range(nkb):
                pend.append(emit_s(kb))
                for fq in finals_q:
                    fq[1] -= 1
                while finals_q and finals_q[0][1] <= 0:
                    finals_q.pop(0)[0]()
                if len(pend) > LOOK:
                    emit_pv(pend.pop(0))
        while pend:
            emit_pv(pend.pop(0))
        while finals_q:
            finals_q.pop(0)[0]()
        orun = Run([("wo", q) for q in range(4)])
        for q in range(4):
            s = orun.get(q)
            for cc in range(2):
                c = 2 * q + cc
                bank = c % 2
                for n in range(NH):
                    off = n * 256 + cc * 128
                    mm(ps[bank][:, :], ring[0:HD, s, off:off + 128], A[0:HD, n, :], n == 0, n == NH - 1,
                       [b_slot[s], b_A[n]], [b_ps[bank]])
                post_chunk(c, bank)
                flush_stats(keep=1)
        post_norm(GI[("mix_post", 1)], False)

    def load_x(i):
        for b in range(4):
            P.dma(SP, lambda h, b=b: h.dma_start(out=xin[:, b, :], in_=x_d[i * T + b * 128:i * T + (b + 1) * 128, :]),
                  reads=[], writes=[b_xin[b]])

    def transpose_in():
        nstate["acc"] = False
        nstate["rstd"] = False
        for k in range(KC):
            for b in range(4):
                P.op(PE, lambda h, k=k, b=b: h.transpose(ps[6][:, b * 128:(b + 1) * 128],
                                                         xin[:, b, k * 128:(k + 1) * 128], ident[:, :]),
                     reads=[b_xin[b], b_const], writes=[b_ps[6]])
            cp(DVE, xT[:, k, :], ps[6][:, :], [b_ps[6]], [b_xT[k]])

    finals = []

    def store_out(i):
        fo = fT[:, :, :].rearrange("p k t -> p (k t)")
        for b in range(4):
            for k4 in range(2):
                for kk in range(4):
                    k = 4 * k4 + kk
                    P.op(PE, lambda h, k=k, b=b, kk=kk: h.transpose(ps[6][:, kk * 128:(kk + 1) * 128],
                                                                     xT[:, k, b * 128:(b + 1) * 128], ident[:, :]),
                         reads=[b_xT[k], b_const], writes=[b_ps[6]])
                cp(DVE, fo[:, b * D + k4 * 512:b * D + (k4 + 1) * 512], ps[6][:, :], [b_ps[6]], [b_fT[2 * b + k4]])
            finals.append(P.dma(ACT, lambda h, b=b: h.dma_start(
                out=out_d[i * T + b * 128:i * T + (b + 1) * 128, :], in_=fo[:, b * D:(b + 1) * D]),
                reads=[b_fT[2 * b], b_fT[2 * b + 1]], writes=[b_out]))

    init()
    load_x(0)
    for i in range(NT):
        cur["tile"] = i
        transpose_in()
        if i > 0 and i + 1 < NT:
            load_x(i + 1)
        ffn(0)
        rg()
        ffn(1)
        kv(i)
        ffn(2)
        attn(i)
        ffn(3)
        if i == 0 and NT > 1:
            load_x(1)
        store_out(i)

    run_prog(nc, P, finals)
    st.close()
    return nc


_CACHE = {}


def kernel(**inputs):
    x = np.ascontiguousarray(np.asarray(inputs["x"], dtype=np.float32))
    n = x.shape[0]
    if "nc" not in _CACHE:
        _CACHE["nc"] = build()
    nc = _CACHE["nc"]
    consts = make_consts()
    shared = {k: np.ascontiguousarray(np.asarray(inputs[k], dtype=np.float32)) for k in WEIGHT_NAMES}
    shared.update(consts)
    in_maps = []
    for b in range(n):
        m = dict(shared)
        m["x"] = x[b]
        in_maps.append(m)
    res = run_bass_kernel_spmd(nc, in_maps, core_ids=list(range(n)))
    return np.stack([np.asarray(r["out"], dtype=np.float32) for r in res.results], axis=0)
```

```python
import contextlib
import numpy as np
import concourse.bass as bass
import concourse.mybir as mybir
from concourse.bass_utils import run_bass_kernel_spmd

F32 = mybir.dt.float32
BF16 = mybir.dt.bfloat16
AF = mybir.ActivationFunctionType
ALU = mybir.AluOpType

PE, ACT, DVE, POOL, SP = "pe", "act", "dve", "pool", "sp"
ENGS = (PE, ACT, DVE, POOL, SP)


class Buf:
    __slots__ = ("name", "lw", "rd")

    def __init__(self, name=""):
        self.name = name
        self.lw = None
        self.rd = []


class Op:
    __slots__ = ("eng", "fn", "waits", "is_dma", "dsem", "dval", "target", "semval")


class Prog:
    def __init__(self, n_dma_sems=32, n_conv_sems=2):
        self.ops = {e: [] for e in ENGS}
        self.n_dma_sems = n_dma_sems
        self.pools = {"main": list(range(0, n_dma_sems - n_conv_sems)),
                      "conv": list(range(n_dma_sems - n_conv_sems, n_dma_sems))}
        self.pool_rr = {"main": 0, "conv": 0}
        self.dma_sem_count = [0] * n_dma_sems
        self.dma_sem_last = [None] * n_dma_sems

    def _mk(self, eng, fn, reads, writes, is_dma, dsem=None):
        op = Op()
        op.eng = eng
        op.fn = fn
        op.is_dma = is_dma
        op.target = False
        op.semval = None
        op.dsem = dsem
        op.dval = None
        deps = []
        for b in reads:
            if b.lw is not None:
                deps.append(("raw", b.lw))
        for b in writes:
            if b.lw is not None:
                deps.append(("waw", b.lw))
            for r in b.rd:
                deps.append(("war", r))
        waits = []
        for kind, src in deps:
            sop = src[1]
            if (not sop.is_dma) and sop.eng == eng and not is_dma:
                if kind != "raw" or eng == PE:
                    continue
            if sop not in waits:
                waits.append(sop)
        op.waits = waits
        self.ops[eng].append(op)
        me = ("d" if is_dma else "e", op)
        for b in reads:
            if is_dma:
                b.rd = [r for r in b.rd if not (r[0] == "d" and r[1].dsem == op.dsem)]
            else:
                b.rd = [r for r in b.rd if not (r[0] == "e" and r[1].eng == eng)]
            b.rd.append(me)
        for b in writes:
            b.lw = me
            b.rd = []
        return op

    def op(self, eng, fn, reads=(), writes=()):
        return self._mk(eng, fn, list(reads), list(writes), False)

    def dma(self, eng, fn, reads=(), writes=(), pool="main"):
        lst = self.pools[pool]
        k = lst[self.pool_rr[pool] % len(lst)]
        self.pool_rr[pool] += 1
        op = self._mk(eng, fn, list(reads), list(writes), True, dsem=k)
        prev = self.dma_sem_last[k]
        if prev is not None and prev not in op.waits:
            op.waits.append(prev)
        self.dma_sem_count[k] += 16
        op.dval = self.dma_sem_count[k]
        self.dma_sem_last[k] = op
        return op


def _emit_engine(prog, e, h, esems, dsems):
    known = {}
    for op in prog.ops[e]:
        need = {}
        for w in op.waits:
            if w.is_dma:
                key, val = ("d", w.dsem), w.dval
            else:
                key, val = ("e", w.eng), w.semval
            if known.get(key, 0) >= val:
                continue
            if need.get(key, 0) < val:
                need[key] = val
        for key, val in need.items():
            sem = dsems[key[1]] if key[0] == "d" else esems[key[1]]
            h.wait_ge(sem, val)
            known[key] = val
        ins = op.fn(h)
        if op.is_dma:
            ins.then_inc(dsems[op.dsem], 16)
        elif op.target:
            ins.then_inc(esems[e], 1)


def run_prog(nc, prog, final_dma_ops):
    for e in ENGS:
        for op in prog.ops[e]:
            for w in op.waits:
                if not w.is_dma:
                    w.target = True
    for e in ENGS:
        c = 0
        for op in prog.ops[e]:
            if op.target and not op.is_dma:
                c += 1
                op.semval = c
    with contextlib.ExitStack() as st:
        esems = {e: st.enter_context(nc.semaphore("s_" + e)) for e in ENGS}
        dsems = [st.enter_context(nc.semaphore("d%d" % i)) for i in range(prog.n_dma_sems)]
        block = st.enter_context(nc.Block())

        def mk(e):
            def body(h):
                _emit_engine(prog, e, h, esems, dsems)
                if e == SP:
                    best = {}
                    for op in final_dma_ops:
                        best[op.dsem] = max(best.get(op.dsem, 0), op.dval)
                    for k, v in best.items():
                        h.wait_ge(dsems[k], v)
            return body

        block.tensor(mk(PE))
        block.scalar(mk(ACT))
        block.vector(mk(DVE))
        block.gpsimd(mk(POOL))
        block.sync(mk(SP))


D = 1024
DFF = 2816
DR = 1344
NB = 16
LB = 84
NH = 16
HD = 64
SEQ = 4096
T = 512
KC = 8
FC = 22
EPS = 1e-6
SLOT = 4096
NSLOT = 6
AUG = 70

WEIGHT_NAMES = [
    "ffn1_pre_g", "ffn1_w_gate", "ffn1_w_up", "ffn1_w_down", "ffn1_post_g", "mix_pre_g", "mix_post_g",
    "ffn2_pre_g", "ffn2_w_gate", "ffn2_w_up", "ffn2_w_down", "ffn2_post_g",
    "rg_w_in", "rg_conv_w", "rg_conv_b", "rg_w_a", "rg_b_a", "rg_w_x", "rg_b_x", "rg_lambda", "rg_w_out",
    "kv_norm_g", "w_kv", "w_fgate", "b_fgate", "attn_w_q", "attn_w_o",
]
WEIGHT_SHAPES = {
    "ffn1_pre_g": [2, D], "ffn1_w_gate": [2, D, DFF], "ffn1_w_up": [2, D, DFF], "ffn1_w_down": [2, DFF, D],
    "ffn1_post_g": [2, D], "mix_pre_g": [2, D], "mix_post_g": [2, D], "ffn2_pre_g": [2, D],
    "ffn2_w_gate": [2, D, DFF], "ffn2_w_up": [2, D, DFF], "ffn2_w_down": [2, DFF, D], "ffn2_post_g": [2, D],
    "rg_w_in": [1, D, 2 * DR], "rg_conv_w": [1, 4, DR], "rg_conv_b": [1, DR], "rg_w_a": [1, NB, LB, LB],
    "rg_b_a": [1, DR], "rg_w_x": [1, NB, LB, LB], "rg_b_x": [1, DR], "rg_lambda": [1, DR],
    "rg_w_out": [1, DR, D], "kv_norm_g": [D], "w_kv": [D, 2 * D], "w_fgate": [D, NH], "b_fgate": [NH],
    "attn_w_q": [1, D, D], "attn_w_o": [1, D, D],
}
GI = {("ffn1_pre", 0): 0, ("ffn1_post", 0): 1, ("mix_pre", 0): 2, ("mix_post", 0): 3, ("ffn2_pre", 0): 4,
      ("ffn2_post", 0): 5, ("kv", 0): 6, ("ffn1_pre", 1): 7, ("ffn1_post", 1): 8, ("mix_pre", 1): 9,
      ("mix_post", 1): 10, ("ffn2_pre", 1): 11, ("ffn2_post", 1): 12}


def make_consts():
    ident = np.eye(128, dtype=np.float32)
    tri = (np.arange(128)[:, None] <= np.arange(128)[None, :]).astype(np.float32)
    selk = np.zeros((128, NH, 6), np.float32)
    selq = np.zeros((128, NH, 6), np.float32)
    for h in range(NH):
        for j in range(3):
            selk[32 * j + h, h, j] = 1.0
            selk[96, h, 3 + j] = 1.0
            selq[96, h, j] = 8.0
            selq[32 * j + h, h, 3 + j] = -8.0
    return {"c_ident": ident, "c_tri": tri, "c_selk": selk.reshape(128, NH * 6), "c_selq": selq.reshape(128, NH * 6)}


def build(ntiles=SEQ // T, dbg_scr=False):
    nc = bass.Bass("TRN2", target_bir_lowering=False)
    P = Prog()
    st = contextlib.ExitStack()
    NT = ntiles

    def dram_in(name, shape):
        return nc.dram_tensor(name, list(shape), F32, kind="ExternalInput").ap()

    x_d = dram_in("x", [SEQ, D])
    W = {n: dram_in(n, WEIGHT_SHAPES[n]) for n in WEIGHT_NAMES}
    c_ident = dram_in("c_ident", [128, 128])
    c_tri = dram_in("c_tri", [128, 128])
    c_selk = dram_in("c_selk", [128, NH * 6])
    c_selq = dram_in("c_selq", [128, NH * 6])
    out_d = nc.dram_tensor("out", [SEQ, D], F32, kind="ExternalOutput").ap()

    def scratch(name, shape):
        return nc.dram_tensor(name, list(shape), BF16, kind="ExternalOutput" if dbg_scr else "Internal").ap()

    scr_gu = scratch("scr_gu", [4, 11, 128, 4096])
    scr_dn = scratch("scr_dn", [4, 4, 2, 128, 2816])
    scr_ri = scratch("scr_ri", [8, 128, 2688])
    scr_ga = scratch("scr_ga", [LB, 2688])
    scr_ro = scratch("scr_ro", [4, LB, 4096])
    scr_wk = scratch("scr_wk", [4, 128, 2048])
    scr_wv = scratch("scr_wv", [2, 128, 4096])
    scr_wf = scratch("scr_wf", [128, 128])
    scr_wq = scratch("scr_wq", [4, 128, 2048])
    scr_wo = scratch("scr_wo", [4, HD, 4096])
    kt_scr = scratch("kt_scr", [NH, AUG, SEQ])
    v_scr = scratch("v_scr", [NH, 128, 32 * 65])

    def sb(name, shape, dt):
        return st.enter_context(nc.sbuf_tensor(name, list(shape), dt))

    xT = sb("xT", [128, KC, T], F32)
    xn = sb("xn", [128, KC, T], BF16)
    A = sb("A", [128, FC, T], BF16)
    fT = sb("fT", [128, KC, T], F32)
    xin = sb("xin", [128, 4, D], F32)
    sq = sb("sq", [128, 3, T], BF16)
    rstd = sb("rstd", [128, T], F32)
    sg = sb("sg", [128, 2, T], F32)
    ring = sb("ring", [128, NSLOT, SLOT], BF16)
    ident = sb("ident", [128, 128], F32)
    ones_bf = sb("ones_bf", [128, 128], BF16)
    ones_f = sb("ones_f", [128, 64], F32)
    ones16 = sb("ones16", [16, T], F32)
    tri = sb("tri", [128, 128], BF16)
    selk = sb("selk", [128, NH * 6], BF16)
    selq = sb("selq", [128, NH * 6], BF16)
    gstage = sb("gstage", [104, 128], F32)
    G = sb("G", [128, 104], F32)
    Gh = sb("Gh", [128, 104], F32)
    rvstage = sb("rvstage", [128, LB], F32)
    RV = sb("RV", [LB, 128], F32)
    RD = sb("RD", [LB, 64], F32)
    nbf = sb("nbf", [16, 2], F32)
    convst = sb("convst", [LB, NB, 3], F32)
    hstate = sb("hstate", [LB, NB], F32)
    cstate = sb("cstate", [16, 1], F32)
    xr2 = sb("xr", [LB, 2, T + 3], F32)
    ycv2 = sb("ycv", [LB, 2, T], F32)
    ybf2 = sb("ybf", [LB, 2, T], BF16)
    b12 = sb("b1", [LB, 2, T], F32)
    b22 = sb("b2", [LB, 2, T], F32)
    b32 = sb("b3", [LB, 2, T], F32)
    b42 = sb("b4", [LB, 2, T], F32)
    hh2 = sb("hh_t", [LB, 2, T], F32)
    g2b = sb("g_sb", [LB, 2, T], F32)
    b1, b2, b3, ybf = b12[:, 0, :], b22[:, 0, :], b32[:, 0, :], ybf2[:, 0, :]
    csplit = sb("csplit", [128, T], BF16)
    KTcur = sb("KTcur", [AUG, NH, T], BF16)
    Vcur = sb("Vcur", [128, NH, 4, 65], BF16)
    QTv = fT[:, :, :].bitcast(BF16)

    def QT(h_):
        return QTv[0:AUG, h_ // 2, (h_ % 2) * T:(h_ % 2 + 1) * T]

    ps = [st.enter_context(nc.psum_tensor("ps%d" % i, [128, T], F32)) for i in range(8)]

    b_xT = [Buf("xT%d" % k) for k in range(KC)]
    b_xn = [Buf("xn%d" % k) for k in range(KC)]
    b_A = [Buf("A%d" % c) for c in range(FC)]
    b_fT = [Buf("fT%d" % c) for c in range(KC)]
    b_xin = [Buf("xin%d" % b) for b in range(4)]
    b_sq = [Buf("sq%d" % i) for i in range(3)]
    b_rstd = Buf("rstd")
    b_sg = [Buf("sg0"), Buf("sg1")]
    b_slot = [Buf("slot%d" % s) for s in range(NSLOT)]
    b_ps = [Buf("ps%d" % i) for i in range(8)]
    b_const = Buf("const")
    b_G, b_RV, b_RD, b_nbf = Buf("G"), Buf("RV"), Buf("RD"), Buf("nbf")
    b_gstage, b_rvstage = Buf("gstage"), Buf("rvstage")
    b_convst = [Buf("convst%d" % j) for j in range(NB)]
    b_hstate = [Buf("hstate%d" % j) for j in range(NB)]
    b_cstate = Buf("cstate")
    bt2 = [{n: Buf(n + str(p)) for n in ["xr", "ycv", "ybf", "b1", "b2", "b3", "b4", "hh", "g"]} for p in range(2)]
    bt = dict(bt2[0])
    bt["csplit"] = Buf("csplit")
    b_KT = [Buf("KT%d" % h) for h in range(NH)]
    b_QT = [b_fT[h // 2] for h in range(NH)]
    b_V = [Buf("V%d" % tb) for tb in range(4)]
    b_PT = [b_A[16 + i] for i in range(4)]
    b_ktscr = [Buf("ktscr%d" % i) for i in range(NT)]
    b_vscr = [Buf("vscr%d" % i) for i in range(NT)]
    b_out = Buf("out")

    def act(out, in_, func, reads, writes, bias=None, scale=None):
        kw = {}
        if bias is not None:
            kw["bias"] = bias
        if scale is not None:
            kw["scale"] = scale
        return P.op(ACT, lambda h: h.activation(out=out, in_=in_, func=func, **kw), reads, writes)

    def tt(eng, out, in0, in1, op, reads, writes):
        return P.op(eng, lambda h: h.tensor_tensor(out=out, in0=in0, in1=in1, op=op), reads, writes)

    def ts(eng, out, in0, s1, s2, op0, op1, reads, writes):
        return P.op(eng, lambda h: h.tensor_scalar(out=out, in0=in0, scalar1=s1, scalar2=s2, op0=op0, op1=op1),
                    reads, writes)

    def stt(out, in0, scalar, in1, op0, op1, reads, writes):
        return P.op(DVE, lambda h: h.scalar_tensor_tensor(out=out, in0=in0, scalar=scalar, in1=in1, op0=op0, op1=op1),
                    reads, writes)

    def cp(eng, out, in_, reads, writes):
        return P.op(eng, lambda h: h.tensor_copy(out=out, in_=in_), reads, writes)

    def mm(out, lhsT, rhs, start, stop, reads, writes, tile_position=None):
        if tile_position is None:
            return P.op(PE, lambda h: h.matmul(out, lhsT=lhsT, rhs=rhs, start=start, stop=stop), reads, writes)
        return P.op(PE, lambda h: h.matmul(out, lhsT=lhsT, rhs=rhs, start=start, stop=stop,
                                           tile_position=tile_position), reads, writes)

    b_scr = {}

    def kp(w2d):
        return w2d.rearrange("(k p) c -> p k c", p=128)

    def ffn_w(f):
        l = f // 2
        pre = "ffn1" if f % 2 == 0 else "ffn2"
        return kp(W[pre + "_w_gate"][l]), kp(W[pre + "_w_up"][l]), kp(W[pre + "_w_down"][l])

    def piece(key):
        kind = key[0]
        if kind == "gu":
            _, f, j = key
            wg, wu, _ = ffn_w(f)
            return scr_gu[f, j], 128, 4096, [(0, wg[:, :, j * 256:(j + 1) * 256]), (2048, wu[:, :, j * 256:(j + 1) * 256])]
        if kind == "dn":
            _, f, q, hf = key
            wd = ffn_w(f)[2]
            return scr_dn[f, q, hf], 128, 2816, [(0, wd[:, hf * 11:(hf + 1) * 11, q * 256:(q + 1) * 256])]
        if kind == "ri":
            p8 = key[1]
            wi = kp(W["rg_w_in"][0])
            return scr_ri[p8], 128, 2688, [(0, wi[:, :, p8 * 168:(p8 + 1) * 168]),
                                           (1344, wi[:, :, DR + p8 * 168:DR + (p8 + 1) * 168])]
        if kind == "ga":
            return scr_ga, LB, 2688, [(0, W["rg_w_a"][0].rearrange("n c d -> c n d")),
                                      (1344, W["rg_w_x"][0].rearrange("n c d -> c n d"))]
        if kind == "ro":
            q = key[1]
            wo = W["rg_w_out"][0].rearrange("(n c) o -> c n o", c=LB)
            return scr_ro[q], LB, 4096, [(0, wo[:, :, q * 256:(q + 1) * 256])]
        if kind == "wk":
            q = key[1]
            return scr_wk[q], 128, 2048, [(0, kp(W["w_kv"])[:, :, q * 256:(q + 1) * 256])]
        if kind == "wv":
            hf = key[1]
            return scr_wv[hf], 128, 4096, [(0, kp(W["w_kv"])[:, :, D + hf * 512:D + (hf + 1) * 512])]
        if kind == "wf":
            return scr_wf, 128, 128, [(0, kp(W["w_fgate"]))]
        if kind == "wq":
            q = key[1]
            return scr_wq[q], 128, 2048, [(0, kp(W["attn_w_q"][0])[:, :, q * 256:(q + 1) * 256])]
        if kind == "wo":
            q = key[1]
            wo = W["attn_w_o"][0].rearrange("(h d) o -> d h o", d=HD)
            return scr_wo[q], HD, 4096, [(0, wo[:, :, q * 256:(q + 1) * 256])]
        raise KeyError(key)

    def init():
        P.dma(SP, lambda h: h.dma_start(out=ident[:], in_=c_ident), writes=[b_const])
        P.dma(SP, lambda h: h.dma_start(out=xin[:, 3, 0:128], in_=c_tri), writes=[b_xin[3]])
        P.dma(SP, lambda h: h.dma_start(out=xin[:, 3, 128:224], in_=c_selk), writes=[b_xin[3]])
        P.dma(SP, lambda h: h.dma_start(out=xin[:, 3, 224:320], in_=c_selq), writes=[b_xin[3]])
        cp(DVE, tri[:], xin[:, 3, 0:128], [b_xin[3]], [b_const])
        cp(DVE, selk[:], xin[:, 3, 128:224], [b_xin[3]], [b_const])
        cp(DVE, selq[:], xin[:, 3, 224:320], [b_xin[3]], [b_const])
        P.op(DVE, lambda h: h.memset(ones_bf[:], 1.0), writes=[b_const])
        P.op(DVE, lambda h: h.memset(ones_f[:], 1.0), writes=[b_const])
        P.op(DVE, lambda h: h.memset(ones16[:], 1.0), writes=[b_const])
        P.op(DVE, lambda h: h.memset(csplit[:], 1.0), writes=[bt["csplit"]])
        P.op(DVE, lambda h: h.memset(Vcur[:], 1.0), writes=b_V)
        P.op(DVE, lambda h: h.memset(convst[:], 0.0), writes=b_convst)
        P.op(DVE, lambda h: h.memset(hstate[:], 0.0), writes=b_hstate)
        P.op(DVE, lambda h: h.memset(cstate[:], 0.0), writes=[b_cstate])
        gsrc = {}
        for (nm, l), gi in GI.items():
            gsrc[gi] = W["kv_norm_g"] if nm == "kv" else W[nm + "_g"][l]
        for gi in range(13):
            P.dma(SP, lambda h, gi=gi: h.dma_start(out=gstage[gi * 8:(gi + 1) * 8, :],
                                                   in_=gsrc[gi].rearrange("(k p) -> k p", p=128)),
                  writes=[b_gstage])
        P.op(PE, lambda h: h.transpose(ps[0][:, 0:104], gstage[:, :], ident[0:104, 0:104]),
             reads=[b_gstage, b_const], writes=[b_ps[0]])
        cp(DVE, G[:], ps[0][:, 0:104], [b_ps[0]], [b_G])
        P.op(DVE, lambda h: h.tensor_scalar_mul(out=Gh[:], in0=G[:], scalar1=0.5), [b_G], [b_G])
        vsrc = [W["rg_conv_w"][0, 0], W["rg_conv_w"][0, 1], W["rg_conv_w"][0, 2], W["rg_conv_w"][0, 3],
                W["rg_conv_b"][0], W["rg_b_a"][0], W["rg_b_x"][0], W["rg_lambda"][0]]
        for v in range(8):
            P.dma(SP, lambda h, v=v: h.dma_start(out=rvstage[v * 16:(v + 1) * 16, :],
                                                 in_=vsrc[v].rearrange("(n c) -> n c", c=LB)),
                  writes=[b_rvstage])
        P.op(PE, lambda h: h.transpose(ps[1][0:LB, 0:128], rvstage[:, :], ident[:, :]),
             reads=[b_rvstage, b_const], writes=[b_ps[1]])
        cp(DVE, RV[:], ps[1][0:LB, 0:128], [b_ps[1]], [b_RV])
        P.op(DVE, lambda h: h.tensor_scalar_mul(out=RD[:, 0:32], in0=RV[:, 80:112], scalar1=0.5), [b_RV], [b_RD])
        act(RD[:, 48:64], RV[:, 112:128], AF.Exp, [b_RV], [b_RD], scale=-1.0)
        act(RD[:, 48:64], RD[:, 48:64], AF.Ln, [b_RD], [b_RD], bias=1.0)
        P.op(DVE, lambda h: h.tensor_scalar_mul(out=RD[:, 32:48], in0=RD[:, 48:64], scalar1=-4.0), [b_RD], [b_RD])
        P.dma(SP, lambda h: h.dma_start(out=nbf[:, 0:1], in_=W["b_fgate"].rearrange("(h o) -> h o", o=1)),
              writes=[b_nbf])
        P.op(DVE, lambda h: h.tensor_scalar_mul(out=nbf[:, 1:2], in0=nbf[:, 0:1], scalar1=-1.0), [b_nbf], [b_nbf])

    cur = {"pl": POOL, "tile": 0}

    def PL():
        return cur["pl"]

    ring_state = {"n": 0}

    def next_slot():
        s_ = ring_state["n"] % NSLOT
        ring_state["n"] += 1
        return s_

    def load_piece(src_ap, npart, nelem, src_bufs):
        s_ = next_slot()
        P.dma(SP, lambda h: h.dma_start(out=ring[0:npart, s_, 0:nelem], in_=src_ap), reads=src_bufs,
              writes=[b_slot[s_]])
        return s_

    cast_rr = {"n": 0, "stg": 0}

    def load_w(key):
        scr_ap, npart, nelem, views = piece(key)
        b = b_scr.setdefault(key, Buf("scr" + str(key)))
        if cur["tile"] > 0:
            return load_piece(scr_ap, npart, nelem, [b])
        s_ = next_slot()
        for off, src in views:
            nrow, ninner = src.shape[1], src.shape[2]
            step = max(1, 1024 // ninner)
            for r0 in range(0, nrow, step):
                rows = min(step, nrow - r0)
                n = rows * ninner
                sgi = cast_rr["stg"] % 4
                cast_rr["stg"] += 1
                stg = xin[0:npart, sgi, 0:n]
                P.dma(SP, lambda h, stg=stg, src=src, r0=r0, rows=rows, ninner=ninner: h.dma_start(
                    out=stg.rearrange("p (r c) -> p r c", c=ninner), in_=src[:, r0:r0 + rows, :]),
                    reads=[], writes=[b_xin[sgi]])
                dst = ring[0:npart, s_, off + r0 * ninner:off + r0 * ninner + n]
                eng = (POOL, ACT, DVE)[cast_rr["n"] % 3]
                cast_rr["n"] += 1
                if eng == ACT:
                    act(dst, stg, AF.Copy, [b_xin[sgi]], [b_slot[s_]])
                else:
                    cp(eng, dst, stg, [b_xin[sgi]], [b_slot[s_]])
        P.dma(ACT, lambda h: h.dma_start(out=scr_ap, in_=ring[0:npart, s_, 0:nelem]), reads=[b_slot[s_]], writes=[b])
        return s_

    class Run:
        LA = 1

        def __init__(self, keys, loader=None):
            self.keys = list(keys)
            self.slots = {}
            self.n = 0
            self.loader = loader or load_w

        def get(self, idx):
            while self.n < len(self.keys) and self.n <= idx + self.LA:
                self.slots[self.n] = self.loader(self.keys[self.n])
                self.n += 1
            return self.slots[idx]

    sq_state = {"n": 0}
    nstate = {"acc": False, "rstd": False}

    def stats_sq(src, src_buf, eng=None):
        i = sq_state["n"] % 3
        sq_state["n"] += 1
        if eng == ACT:
            act(sq[:, i, :], src, AF.Square, [src_buf], [b_sq[i]])
        else:
            tt(PL(), sq[:, i, :], src, src, ALU.mult, [src_buf], [b_sq[i]])

        def acc(first, last):
            mm(ps[7][:, :], ones_bf[:, :], sq[:, i, :], first, last, [b_const, b_sq[i]], [b_ps[7]])
        return acc

    def stats_finish():
        act(rstd[:], ps[7][:, :], AF.Sqrt, [b_ps[7]], [b_rstd], bias=EPS, scale=1.0 / D)
        P.op(DVE, lambda h: h.reciprocal(out=rstd[:], in_=rstd[:]), [b_rstd], [b_rstd])

    def pre_norm(gi):
        if not nstate["rstd"]:
            if not nstate["acc"]:
                for k in range(KC):
                    stats_sq(xT[:, k, :], b_xT[k], ACT)(k == 0, k == KC - 1)
            stats_finish()
        nstate["acc"] = False
        nstate["rstd"] = True
        for k in range(KC):
            stt(xn[:, k, :], xT[:, k, :], G[:, gi * 8 + k:gi * 8 + k + 1], rstd[:], ALU.mult, ALU.mult,
                [b_xT[k], b_G, b_rstd], [b_xn[k]])

    pend_stats = []

    def post_chunk(c, bank, npart=128):
        act(fT[:, c, :], ps[bank][:, :], AF.Copy, [b_ps[bank]], [b_fT[c]])
        i = sq_state["n"] % 3
        sq_state["n"] += 1
        act(sq[:, i, :], ps[bank][:, :], AF.Square, [b_ps[bank]], [b_sq[i]])
        pend_stats.append(lambda: mm(ps[7][:, :], ones_bf[:, :], sq[:, i, :], c == 0, c == KC - 1,
                                     [b_const, b_sq[i]], [b_ps[7]]))

    def flush_stats(keep=0):
        while len(pend_stats) > keep:
            pend_stats.pop(0)()

    def post_norm(gi, half, next_pre=True):
        Gx = Gh if half else G
        flush_stats()
        stats_finish()
        for c in range(KC):
            tt(PL() if c % 2 == 0 else DVE, fT[:, c, :], fT[:, c, :], rstd[:], ALU.mult, [b_fT[c], b_rstd], [b_fT[c]])
            stt(xT[:, c, :], fT[:, c, :], Gx[:, gi * 8 + c:gi * 8 + c + 1], xT[:, c, :], ALU.mult, ALU.add,
                [b_fT[c], b_G, b_xT[c]], [b_xT[c]])
            if next_pre:
                stats_sq(xT[:, c, :], b_xT[c], ACT)(c == 0, c == KC - 1)
        nstate["acc"] = next_pre
        nstate["rstd"] = False

    def ffn(f):
        l = f // 2
        nm = "ffn1" if f % 2 == 0 else "ffn2"
        run = Run([("gu", f, j) for j in range(11)] + [("dn", f, q, hf) for q in range(4) for hf in range(2)])
        run.LA = 2
        run.get(0)
        pre_norm(GI[(nm + "_pre", l)])
        for j in range(11):
            s = run.get(j)
            for hc in range(2):
                m = 2 * j + hc
                bg, bu = (0, 1) if m % 2 == 0 else (2, 3)
                for a, bank in ((0, bg), (1, bu)):
                    for k in range(KC):
                        off = (a * KC + k) * 256 + hc * 128
                        mm(ps[bank][:, :], ring[:, s, off:off + 128], xn[:, k, :], k == 0, k == KC - 1,
                           [b_slot[s], b_xn[k]], [b_ps[bank]])
                act(sg[:, m % 2, :], ps[bg][:, :], AF.Silu, [b_ps[bg]], [b_sg[m % 2]])
                tt(DVE, A[:, m, :], sg[:, m % 2, :], ps[bu][:, :], ALU.mult, [b_sg[m % 2], b_ps[bu]], [b_A[m]])
        for q in range(4):
            sh = [run.get(11 + 2 * q), run.get(11 + 2 * q + 1)]
            for cc in range(2):
                c = 2 * q + cc
                bank = 4 + c % 2
                for k in range(FC):
                    s = sh[k // 11]
                    off = (k % 11) * 256 + cc * 128
                    mm(ps[bank][:, :], ring[:, s, off:off + 128], A[:, k, :], k == 0, k == FC - 1,
                       [b_slot[s], b_A[k]], [b_ps[bank]])
                post_chunk(c, bank)
                flush_stats(keep=1)
        post_norm(GI[(nm + "_post", l)], True, next_pre=(f != 3))

    def rg():
        def ga_again(key):
            return load_piece(scr_ga, LB, 2688, [b_scr[("ga",)]])
        run = Run([("ga",)] + [("ri", p) for p in range(4)] + [("ga2",)] + [("ri", p) for p in range(4, 8)],
                  loader=lambda key: ga_again(key) if key[0] == "ga2" else load_w(key))
        run.get(0)
        pre_norm(GI[("mix_pre", 0)])
        ro_run = Run([("ro", q) for q in range(4)])

        def views(j):
            par = j % 2
            return (bt2[par], xr2[:, par, :], ycv2[:, par, :], ybf2[:, par, :], b12[:, par, :], b22[:, par, :],
                    b32[:, par, :], b42[:, par, :], hh2[:, par, :], g2b[:, par, :])

        def phase_a(j):
            btp, xr, ycv, ybf_, b1_, b2_, b3_, b4_, hh_t, g_sb = views(j)
            p8, jj = j // 2, j % 2
            s = run.get(1 + p8 if p8 < 4 else 2 + p8)
            sga = run.get(0 if p8 < 4 else 5)
            bgate, brec = (0, 1) if j % 2 == 0 else (2, 3)
            for a_, bank in ((0, bgate), (1, brec)):
                for k in range(KC):
                    off = (a_ * KC + k) * 168 + jj * LB
                    mm(ps[bank][0:LB, :], ring[:, s, off:off + LB], xn[:, k, :], k == 0, k == KC - 1,
                       [b_slot[s], b_xn[k]], [b_ps[bank]])
            cp(PL(), xr[:, 0:3], convst[:, j, :], [b_convst[j]], [btp["xr"]])
            act(xr[:, 3:3 + T], ps[brec][0:LB, :], AF.Copy, [b_ps[brec]], [btp["xr"]])
            act(g_sb, ps[bgate][0:LB, :], AF.Copy, [b_ps[bgate]], [btp["g"]])
            cp(PL(), convst[:, j, :], xr[:, T:T + 3], [btp["xr"]], [b_convst[j]])
            ts(DVE, ycv, xr[:, 0:T], RV[:, 0 * 16 + j:0 * 16 + j + 1], RV[:, 4 * 16 + j:4 * 16 + j + 1],
               ALU.mult, ALU.add, [btp["xr"], b_RV], [btp["ycv"]])
            for tap in range(1, 4):
                stt(ycv, xr[:, tap:tap + T], RV[:, tap * 16 + j:tap * 16 + j + 1], ycv, ALU.mult, ALU.add,
                    [btp["xr"], b_RV, btp["ycv"]], [btp["ycv"]])
            cp(PL(), ybf_, ycv, [btp["ycv"]], [btp["ybf"]])
            gr, gi_ = (4, 5) if j % 2 == 0 else (6, 7)
            mm(ps[gr][0:LB, :], ring[0:LB, sga, (0 * NB + j) * LB:(0 * NB + j + 1) * LB], ybf_, True, True,
               [b_slot[sga], btp["ybf"]], [b_ps[gr]])
            mm(ps[gi_][0:LB, :], ring[0:LB, sga, (1 * NB + j) * LB:(1 * NB + j + 1) * LB], ybf_, True, True,
               [b_slot[sga], btp["ybf"]], [b_ps[gi_]])
            tt(PL(), b4_, g_sb, g_sb, ALU.mult, [btp["g"]], [btp["b4"]])
            ts(PL(), b4_, b4_, 0.044715, 1.0, ALU.mult, ALU.add, [btp["b4"]], [btp["b4"]])
            tt(PL(), b4_, b4_, g_sb, ALU.mult, [btp["b4"], btp["g"]], [btp["b4"]])

        def phase_a2(j):
            btp, xr, ycv, ybf_, b1_, b2_, b3_, b4_, hh_t, g_sb = views(j)
            gr, gi_ = (4, 5) if j % 2 == 0 else (6, 7)
            act(b1_, ps[gr][0:LB, :], AF.Tanh, [b_ps[gr], b_RD], [btp["b1"]], bias=RD[:, j:j + 1], scale=0.5)
            act(b3_, ps[gi_][0:LB, :], AF.Tanh, [b_ps[gi_], b_RD], [btp["b3"]], bias=RD[:, 16 + j:17 + j], scale=0.5)

        def phase_b(j):
            btp, xr, ycv, ybf_, b1_, b2_, b3_, b4_, hh_t, g_sb = views(j)
            act(b1_, b1_, AF.Exp, [btp["b1"], b_RD], [btp["b1"]], bias=RD[:, 32 + j:33 + j], scale=RD[:, 32 + j:33 + j])
            act(b4_, b4_, AF.Tanh, [btp["b4"]], [btp["b4"]], scale=0.7978845608028654)
            act(b2_, b1_, AF.Square, [btp["b1"]], [btp["b2"]])
            act(b2_, b2_, AF.Sqrt, [btp["b2"]], [btp["b2"]], bias=0.25, scale=-0.25)
            stt(b3_, b3_, 1.0, ycv, ALU.add, ALU.mult, [btp["b3"], btp["ycv"]], [btp["b3"]])
            stt(b4_, b4_, 1.0, g_sb, ALU.add, ALU.mult, [btp["b4"], btp["g"]], [btp["b4"]])
            tt(PL(), b2_, b2_, b3_, ALU.mult, [btp["b2"], btp["b3"]], [btp["b2"]])
            P.op(DVE, lambda h, j=j, hh_t=hh_t, b1_=b1_, b2_=b2_: h.tensor_tensor_scan(
                out=hh_t, data0=b1_, data1=b2_, initial=hstate[:, j:j + 1], op0=ALU.mult, op1=ALU.add),
                [btp["b1"], btp["b2"], b_hstate[j]], [btp["hh"]])
            cp(PL(), hstate[:, j:j + 1], hh_t[:, T - 1:T], [btp["hh"]], [b_hstate[j]])
            stt(A[0:LB, j, :], b4_, 0.5, hh_t, ALU.mult, ALU.mult, [btp["b4"], btp["hh"]], [b_A[j]])

        phase_a(0)
        phase_a2(0)
        for j in range(NB):
            if j + 1 < NB:
                phase_a(j + 1)
            else:
                ro_run.get(0)
            phase_b(j)
            if j + 1 < NB:
                phase_a2(j + 1)
        for q in range(4):
            s = ro_run.get(q)
            for cc in range(2):
                c = 2 * q + cc
                bank = c % 2
                for n in range(NB):
                    off = n * 256 + cc * 128
                    mm(ps[bank][:, :], ring[0:LB, s, off:off + 128], A[0:LB, n, :], n == 0, n == NB - 1,
                       [b_slot[s], b_A[n]], [b_ps[bank]])
                post_chunk(c, bank)
                flush_stats(keep=1)
        post_norm(GI[("mix_post", 0)], False)

    def kv(i):
        run = Run([("wf",)] + [("wk", q) for q in range(4)] + [("wv", hf) for hf in range(2)])
        run.get(0)
        pre_norm(GI[("kv", 0)])
        s = run.get(0)
        for k in range(KC):
            mm(ps[4][0:16, :], ring[:, s, k * 16:(k + 1) * 16], xn[:, k, :], k == 0, k == KC - 1,
               [b_slot[s], b_xn[k]], [b_ps[4]])
        fl_e, fl_sp, cneg, c_hi, c_mid = b1[0:16, :], b2[0:16, :], b3[0:16, :], ybf[0:16, :], A[0:16, 21, :]
        c_r1, c_r2 = b1[0:16, :], b2[0:16, :]
        act(fl_e, ps[4][0:16, :], AF.Exp, [b_ps[4], b_nbf], [bt["b1"]], bias=nbf[:, 1:2], scale=-1.0)
        act(fl_sp, fl_e, AF.Ln, [bt["b1"]], [bt["b2"]], bias=1.0)
        P.op(DVE, lambda h: h.tensor_tensor_scan(out=cneg, data0=ones16[:], data1=fl_sp,
                                                 initial=cstate[:, 0:1], op0=ALU.mult, op1=ALU.add),
             [b_const, bt["b2"], b_cstate], [bt["b3"]])
        cp(PL(), cstate[:, 0:1], b3[0:16, T - 1:T], [bt["b3"]], [b_cstate])
        cp(DVE, c_hi, cneg, [bt["b3"]], [bt["ybf"]])
        tt(DVE, c_r1, cneg, c_hi, ALU.subtract, [bt["b3"], bt["ybf"]], [bt["b1"]])
        cp(DVE, c_mid, c_r1, [bt["b1"]], [b_A[21]])
        tt(DVE, c_r2, c_r1, c_mid, ALU.subtract, [bt["b1"], b_A[21]], [bt["b2"]])
        cp(PL(), csplit[0:16, :], c_hi, [bt["ybf"]], [bt["csplit"]])
        cp(PL(), csplit[32:48, :], c_mid, [b_A[21]], [bt["csplit"]])
        cp(DVE, csplit[64:80, :], c_r2, [bt["b2"]], [bt["csplit"]])
        for q in range(4):
            s = run.get(1 + q)
            for hh in range(4):
                h_ = 4 * q + hh
                bank = h_ % 2
                for k in range(KC):
                    off = k * 256 + hh * HD
                    mm(ps[bank][0:HD, :], ring[:, s, off:off + HD], xn[:, k, :], k == 0, k == KC - 1,
                       [b_slot[s], b_xn[k]], [b_ps[bank]])
                mm(ps[bank][HD:AUG, :], selk[:, h_ * 6:(h_ + 1) * 6], csplit[:, :], True, True,
                   [b_const, bt["csplit"]], [b_ps[bank]], tile_position=(0, HD))
                act(KTcur[:, h_, :], ps[bank][0:AUG, :], AF.Copy, [b_ps[bank]], [b_KT[h_]])
        for hf in range(2):
            s = run.get(5 + hf)
            for tb in range(4):
                bank = 2 + tb % 2
                for k in range(KC):
                    mm(ps[bank][:, :], xn[:, k, tb * 128:(tb + 1) * 128], ring[:, s, k * 512:(k + 1) * 512],
                       k == 0, k == KC - 1, [b_slot[s], b_xn[k]], [b_ps[bank]])
                cp(DVE, Vcur[:, hf * 8:(hf + 1) * 8, tb, 0:HD], ps[bank][:, :].rearrange("p (h d) -> p h d", h=8),
                   [b_ps[bank]], [b_V[tb]])
        if i < NT - 1:
            P.dma(ACT, lambda h: h.dma_start(out=kt_scr[:, :, i * T:(i + 1) * T].rearrange("h r c -> r h c"),
                                             in_=KTcur[:, :, :]), reads=b_KT, writes=[b_ktscr[i]])
            P.dma(ACT, lambda h: h.dma_start(
                out=v_scr[:, :, i * 260:(i + 1) * 260].rearrange("h p e -> p h e"),
                in_=Vcur[:, :, :, :].rearrange("p h b d -> p h (b d)")), reads=b_V, writes=[b_vscr[i]])

    st_rot = {"s": 0, "p": 0}

    def attn(i):
        qrun = Run([("wq", q) for q in range(4)])
        qrun.get(0)
        pre_norm(GI[("mix_pre", 1)])
        nprev = 4 * i
        sq_slot = None
        for h_ in range(NH):
            if h_ % 4 == 0:
                sq_slot = qrun.get(h_ // 4)
            hh = h_ % 4
            qb = h_ % 2
            for k in range(KC):
                off = k * 256 + hh * HD
                mm(ps[qb][0:HD, :], ring[:, sq_slot, off:off + HD], xn[:, k, :], k == 0, k == KC - 1,
                   [b_slot[sq_slot], b_xn[k]], [b_ps[qb]])
            mm(ps[qb][HD:AUG, :], selq[:, h_ * 6:(h_ + 1) * 6], csplit[:, :], True, True,
               [b_const, bt["csplit"]], [b_ps[qb]], tile_position=(0, HD))
            act(QT(h_), ps[qb][0:AUG, :], AF.Copy, [b_ps[qb]], [b_QT[h_]])
        LOOK = 2
        pend_final = []
        for h_ in range(NH):
            if i > 0:
                sk = load_piece(kt_scr[h_, :, 0:T * i], AUG, T * i, b_ktscr[0:i])
                sv = load_piece(v_scr[h_, :, 0:260 * i], 128, 260 * i, b_vscr[0:i])
            ob = 5 + h_ % 2
            nkb = nprev + 4

            def emit_s(kb):
                sbank = 2 + st_rot["s"] % 3
                st_rot["s"] += 1
                pi = st_rot["p"] % 4
                st_rot["p"] += 1
                if kb < nprev:
                    kt_ap, kt_b = ring[0:AUG, sk, kb * 128:(kb + 1) * 128], [b_slot[sk]]
                    v_ap, v_b = ring[:, sv, kb * 65:(kb + 1) * 65], [b_slot[sv]]
                    j = -1
                else:
                    j = kb - nprev
                    kt_ap, kt_b = KTcur[:, h_, j * 128:(j + 1) * 128], [b_KT[h_]]
                    v_ap, v_b = Vcur[:, h_, j, :], [b_V[j]]
                mm(ps[sbank][:, :], kt_ap, QT(h_), True, True, kt_b + [b_QT[h_]], [b_ps[sbank]])
                if j <= 0:
                    act(A[:, 16 + pi, :], ps[sbank][:, :], AF.Exp, [b_ps[sbank]], [b_PT[pi]], scale=0.125)
                else:
                    P.op(PL(), lambda h, pi=pi, j=j: h.memset(A[:, 16 + pi, 0:128 * j], 0.0), [], [b_PT[pi]])
                    act(A[:, 16 + pi, 128 * j:T], ps[sbank][:, 128 * j:T], AF.Exp, [b_ps[sbank]], [b_PT[pi]], scale=0.125)
                if j >= 0:
                    tt(DVE, A[:, 16 + pi, 128 * j:128 * (j + 1)], A[:, 16 + pi, 128 * j:128 * (j + 1)], tri[:, :],
                       ALU.mult, [b_PT[pi], b_const], [b_PT[pi]])
                return (kb, pi, v_ap, v_b)

            def emit_pv(item, ob=ob, nkb=nkb):
                kb, pi, v_ap, v_b = item
                mm(ps[ob][0:65, :], v_ap, A[:, 16 + pi, :], kb == 0, kb == nkb - 1, v_b + [b_PT[pi]], [b_ps[ob]])

            def finalize(h_=h_, ob=ob):
                P.op(DVE, lambda h: h.reciprocal(out=b1[64:65, :], in_=ps[ob][64:65, :]), [b_ps[ob]], [bt["b1"]])
                mm(ps[7][0:HD, :], ones_f[64:65, 0:HD], b1[64:65, :], True, True, [b_const, bt["b1"]], [b_ps[7]],
                   tile_position=(64, 0))
                act(b2[0:HD, :], ps[7][0:HD, :], AF.Copy, [b_ps[7]], [bt["b2"]])
                tt(DVE, A[0:HD, h_, :], ps[ob][0:HD, :], b2[0:HD, :], ALU.mult, [b_ps[ob], bt["b2"]], [b_A[h_]])

            pend = []
            for kb in range(nkb):
                pend.append(emit_s(kb))
                if kb == LOOK - 1 and pend_final:
                    pend_final.pop(0)()
                if len(pend) > LOOK:
                    emit_pv(pend.pop(0))
            while pend:
                emit_pv(pend.pop(0))
            pend_final.append(finalize)
        while pend_final:
            pend_final.pop(0)()
        orun = Run([("wo", q) for q in range(4)])
        for q in range(4):
            s = orun.get(q)
            for cc in range(2):
                c = 2 * q + cc
                bank = c % 2
                for n in range(NH):
                    off = n * 256 + cc * 128
                    mm(ps[bank][:, :], ring[0:HD, s, off:off + 128], A[0:HD, n, :], n == 0, n == NH - 1,
                       [b_slot[s], b_A[n]], [b_ps[bank]])
                post_chunk(c, bank)
                flush_stats(keep=1)
        post_norm(GI[("mix_post", 1)], False)

    def load_x(i):
        for b in range(4):
            P.dma(SP, lambda h, b=b: h.dma_start(out=xin[:, b, :], in_=x_d[i * T + b * 128:i * T + (b + 1) * 128, :]),
                  reads=[], writes=[b_xin[b]])

    def transpose_in():
        nstate["acc"] = False
        nstate["rstd"] = False
        for k in range(KC):
            for b in range(4):
                P.op(PE, lambda h, k=k, b=b: h.transpose(ps[6][:, b * 128:(b + 1) * 128],
                                                         xin[:, b, k * 128:(k + 1) * 128], ident[:, :]),
                     reads=[b_xin[b], b_const], writes=[b_ps[6]])
            cp(DVE, xT[:, k, :], ps[6][:, :], [b_ps[6]], [b_xT[k]])

    finals = []

    def store_out(i):
        fo = fT[:, :, :].rearrange("p k t -> p (k t)")
        for b in range(4):
            for k4 in range(2):
                for kk in range(4):
                    k = 4 * k4 + kk
                    P.op(PE, lambda h, k=k, b=b, kk=kk: h.transpose(ps[6][:, kk * 128:(kk + 1) * 128],
                                                                     xT[:, k, b * 128:(b + 1) * 128], ident[:, :]),
                         reads=[b_xT[k], b_const], writes=[b_ps[6]])
                cp(DVE, fo[:, b * D + k4 * 512:b * D + (k4 + 1) * 512], ps[6][:, :], [b_ps[6]], [b_fT[2 * b + k4]])
            finals.append(P.dma(ACT, lambda h, b=b: h.dma_start(
                out=out_d[i * T + b * 128:i * T + (b + 1) * 128, :], in_=fo[:, b * D:(b + 1) * D]),
                reads=[b_fT[2 * b], b_fT[2 * b + 1]], writes=[b_out]))

    init()
    load_x(0)
    for i in range(NT):
        cur["tile"] = i
        transpose_in()
        if i > 0 and i + 1 < NT:
            load_x(i + 1)
        ffn(0)
        rg()
        ffn(1)
        kv(i)
        ffn(2)
        attn(i)
        ffn(3)
        if i == 0 and NT > 1:
            load_x(1)
        store_out(i)

    run_prog(nc, P, finals)
    st.close()
    return nc


_CACHE = {}


def kernel(**inputs):
    x = np.ascontiguousarray(np.asarray(inputs["x"], dtype=np.float32))
    n = x.shape[0]
    if "nc" not in _CACHE:
        _CACHE["nc"] = build()
    nc = _CACHE["nc"]
    consts = make_consts()
    shared = {k: np.ascontiguousarray(np.asarray(inputs[k], dtype=np.float32)) for k in WEIGHT_NAMES}
    shared.update(consts)
    in_maps = []
    for b in range(n):
        m = dict(shared)
        m["x"] = x[b]
        in_maps.append(m)
    res = run_bass_kernel_spmd(nc, in_maps, core_ids=list(range(n)))
    return np.stack([np.asarray(r["out"], dtype=np.float32) for r in res.results], axis=0)
```
